# Optimizing a Trainium2 kernel written in Bass

```python
import jax
import jax.numpy as jnp
from jax import lax
import numpy as np

D_MODEL = 2048
BATCH = 2
SEQ = 4096
DEPTH = 1

MEM_LEN = 256
EPS = 1e-6
MLSTM_HEADS = 4
MLSTM_QK = 128
MLSTM_V = 256
MLSTM_CHUNK = 64
CONV_WIDTH = 4
HGRN_HEADS = 8
HGRN_DK = 128
HGRN_DV = 128
HGRN_CHUNK = 16
XATTN_HEADS = 4
XATTN_HEAD_DIM = D_MODEL // XATTN_HEADS
D_FF = 5632

MLSTM_QK_W = MLSTM_HEADS * MLSTM_QK
MLSTM_V_W = MLSTM_HEADS * MLSTM_V
HGRN_K_W = HGRN_HEADS * HGRN_DK
HGRN_V_W = HGRN_HEADS * HGRN_DV
SPLIT_SIZES = (MLSTM_QK_W, MLSTM_QK_W, MLSTM_V_W, MLSTM_V_W, MLSTM_HEADS, MLSTM_HEADS,
               HGRN_K_W, HGRN_K_W, HGRN_V_W, HGRN_V_W, D_MODEL, D_MODEL)
D_IN = (2 * MLSTM_QK_W + 2 * MLSTM_V_W + 2 * MLSTM_HEADS + 2 * HGRN_K_W + 2 * HGRN_V_W + 2 * D_MODEL)

kernel_name = 'hybrid_mlstm_hgrn2_macaron_block'


def rmsnorm(x, g):
    xf = x.astype(jnp.float32)
    y = xf * lax.rsqrt(jnp.mean(xf * xf, axis=-1, keepdims=True) + EPS)
    return (y * g.astype(jnp.float32)).astype(x.dtype)


def head_rmsnorm(h, g):
    n_h, d = h.shape[1], h.shape[3]
    y = h * lax.rsqrt(jnp.mean(h * h, axis=-1, keepdims=True) + EPS)
    return y * g.astype(jnp.float32).reshape(1, n_h, 1, d)


def split_heads(t, n_heads):
    b, s, _ = t.shape
    return t.reshape(b, s, n_heads, -1).transpose(0, 2, 1, 3)


def merge_heads(t):
    b, h, s, d = t.shape
    return t.transpose(0, 2, 1, 3).reshape(b, s, h * d)


def to_chunks(t, chunk):
    b, h, s = t.shape[:3]
    return jnp.moveaxis(t.reshape((b, h, s // chunk, chunk) + t.shape[3:]), 2, 0)


def from_chunks(t):
    nc, b, h, l, d = t.shape
    return jnp.moveaxis(t, 0, 2).reshape(b, h, nc * l, d)


def causal_conv(x, w, b):
    c = x.shape[-1]
    y = lax.conv_general_dilated(x, w[:, None, :].astype(x.dtype), window_strides=(1,),
                                 padding=[(CONV_WIDTH - 1, 0)],
                                 dimension_numbers=('NWC', 'WIO', 'NWC'), feature_group_count=c)
    return y + b.astype(x.dtype)


def swiglu(h, w1, w3, w2):
    return (jax.nn.silu(h @ w1) * (h @ w3)) @ w2


def mlstm_chunkwise(q, k, v, i_pre, f_log):
    b_, h_, _, dk = q.shape
    dv = v.shape[-1]
    causal = jnp.tril(jnp.ones((MLSTM_CHUNK, MLSTM_CHUNK), dtype=bool))

    def step(carry, inp):
        c_state, n_state, m_state = carry
        qc, kc, vc, ic, fc = inp
        b = jnp.cumsum(fc, axis=-1)
        d_log = jnp.where(causal, b[..., :, None] - b[..., None, :] + ic[..., None, :], -jnp.inf)
        inter_log = b + m_state[..., None]
        m_t = jnp.maximum(jnp.max(d_log, axis=-1), inter_log)
        s = jnp.einsum('bhtd,bhsd->bhts', qc, kc) * jnp.exp(d_log - m_t[..., None])
        w_inter = jnp.exp(inter_log - m_t)
        num = jnp.einsum('bhts,bhse->bhte', s, vc) + w_inter[..., None] * jnp.einsum('bhtd,bhde->bhte', qc, c_state)
        den = jnp.sum(s, axis=-1) + w_inter * jnp.einsum('bhtd,bhd->bht', qc, n_state)
        h = num / jnp.maximum(jnp.abs(den), jnp.exp(-m_t))[..., None]
        b_last = b[..., -1]
        a_log = b_last[..., None] - b + ic
        m_new = jnp.maximum(b_last + m_state, jnp.max(a_log, axis=-1))
        w_a = jnp.exp(a_log - m_new[..., None])
        decay = jnp.exp(b_last + m_state - m_new)
        c_state = decay[..., None, None] * c_state + jnp.einsum('bhs,bhsd,bhse->bhde', w_a, kc, vc)
        n_state = decay[..., None] * n_state + jnp.einsum('bhs,bhsd->bhd', w_a, kc)
        return (c_state, n_state, m_new), h

    init = (jnp.zeros((b_, h_, dk, dv), jnp.float32), jnp.zeros((b_, h_, dk), jnp.float32),
            jnp.zeros((b_, h_), jnp.float32))
    xs = (to_chunks(q, MLSTM_CHUNK), to_chunks(k, MLSTM_CHUNK), to_chunks(v, MLSTM_CHUNK),
          to_chunks(i_pre, MLSTM_CHUNK), to_chunks(f_log, MLSTM_CHUNK))
    _, h = lax.scan(step, init, xs)
    return from_chunks(h)


def hgrn2_chunkwise(q, k, v, g_log):
    b_, h_, _, dk = q.shape
    dv = v.shape[-1]
    causal = jnp.tril(jnp.ones((HGRN_CHUNK, HGRN_CHUNK), dtype=bool))

    def step(state, inp):
        qc, kc, vc, gc = inp
        g = jnp.cumsum(gc, axis=-2)
        diff = jnp.where(causal[:, :, None], g[..., :, None, :] - g[..., None, :, :], -jnp.inf)
        a = jnp.einsum('bhtd,bhsd,bhtsd->bhts', qc, kc, jnp.exp(diff))
        o = jnp.einsum('bhts,bhse->bhte', a, vc) + jnp.einsum('bhtd,bhde->bhte', qc * jnp.exp(g), state)
        g_last = g[..., -1, :]
        state = jnp.exp(g_last)[..., None] * state + jnp.einsum(
            'bhsd,bhse->bhde', kc * jnp.exp(g_last[..., None, :] - g), vc)
        return state, o

    init = jnp.zeros((b_, h_, dk, dv), jnp.float32)
    xs = (to_chunks(q, HGRN_CHUNK), to_chunks(k, HGRN_CHUNK), to_chunks(v, HGRN_CHUNK),
          to_chunks(g_log, HGRN_CHUNK))
    _, o = lax.scan(step, init, xs)
    return from_chunks(o)


def mlstm_mixer(q_raw, k_raw, v_raw, o_raw, i_raw, f_raw, conv_w, conv_b, ig_bias, fg_bias, head_norm):
    qk = jax.nn.silu(causal_conv(jnp.concatenate([q_raw, k_raw], axis=-1), conv_w, conv_b)).astype(jnp.float32)
    q, k = jnp.split(qk, 2, axis=-1)
    q = split_heads(q, MLSTM_HEADS)
    k = split_heads(k, MLSTM_HEADS) * (MLSTM_QK ** -0.5)
    v = split_heads(v_raw.astype(jnp.float32), MLSTM_HEADS)
    i_pre = (i_raw.astype(jnp.float32) + ig_bias.astype(jnp.float32)).transpose(0, 2, 1)
    f_log = jax.nn.log_sigmoid(f_raw.astype(jnp.float32) + fg_bias.astype(jnp.float32)).transpose(0, 2, 1)
    h = mlstm_chunkwise(q, k, v, i_pre, f_log)
    h = merge_heads(head_rmsnorm(h, head_norm))
    return h * jax.nn.sigmoid(o_raw.astype(jnp.float32))


def hgrn2_mixer(q_raw, f_raw, i_raw, g_raw, lb, head_norm):
    q = split_heads(jax.nn.silu(q_raw.astype(jnp.float32)), HGRN_HEADS) * (HGRN_DK ** -0.5)
    f = lb + (1.0 - lb) * jax.nn.sigmoid(f_raw.astype(jnp.float32))
    k = split_heads(1.0 - f, HGRN_HEADS)
    g_log = split_heads(jnp.log(f), HGRN_HEADS)
    v = split_heads(i_raw.astype(jnp.float32), HGRN_HEADS)
    o = hgrn2_chunkwise(q, k, v, g_log)
    o = merge_heads(head_rmsnorm(o, head_norm))
    return o * jax.nn.silu(g_raw.astype(jnp.float32))


def cross_attention(xn, memn, w_q, w_kv, w_o):
    b, s, _ = xn.shape
    m = memn.shape[1]
    q = (xn @ w_q).reshape(b, s, XATTN_HEADS, XATTN_HEAD_DIM)
    k, v = jnp.split(memn @ w_kv, 2, axis=-1)
    k = k.reshape(b, m, XATTN_HEADS, XATTN_HEAD_DIM)
    v = v.reshape(b, m, XATTN_HEADS, XATTN_HEAD_DIM)
    scores = jnp.einsum('bthd,bmhd->bhtm', q, k).astype(jnp.float32) * (XATTN_HEAD_DIM ** -0.5)
    p = jax.nn.softmax(scores, axis=-1).astype(v.dtype)
    o = jnp.einsum('bhtm,bmhd->bthd', p, v).reshape(b, s, D_MODEL)
    return o @ w_o


def setup_inputs(seed: int = 0) -> dict:
    key = jax.random.key(seed)
    ks = jax.random.split(key, 32)

    def normal(k, shape, scale):
        return scale * jax.random.normal(k, shape, jnp.float32)

    def gain(k, shape):
        return 1.0 + normal(k, shape, 0.02)

    fg_base = jnp.linspace(3.0, 6.0, MLSTM_HEADS, dtype=jnp.float32)[None, :]
    return {
        'x': normal(ks[0], (BATCH, SEQ, D_MODEL), 1.0),
        'mem': normal(ks[1], (BATCH, MEM_LEN, D_MODEL), 1.0),
        'norm_ffn1': gain(ks[2], (DEPTH, D_MODEL)),
        'ffn1_w1': normal(ks[3], (DEPTH, D_MODEL, D_FF), D_MODEL ** -0.5),
        'ffn1_w3': normal(ks[4], (DEPTH, D_MODEL, D_FF), D_MODEL ** -0.5),
        'ffn1_w2': normal(ks[5], (DEPTH, D_FF, D_MODEL), D_FF ** -0.5),
        'norm_mix': gain(ks[6], (DEPTH, D_MODEL)),
        'w_in': normal(ks[7], (DEPTH, D_MODEL, D_IN), D_MODEL ** -0.5),
        'mlstm_conv_w': normal(ks[8], (DEPTH, CONV_WIDTH, 2 * MLSTM_QK_W), CONV_WIDTH ** -0.5),
        'mlstm_conv_b': normal(ks[9], (DEPTH, 2 * MLSTM_QK_W), 0.02),
        'mlstm_ig_bias': normal(ks[10], (DEPTH, MLSTM_HEADS), 0.1),
        'mlstm_fg_bias': fg_base + normal(ks[11], (DEPTH, MLSTM_HEADS), 0.01),
        'mlstm_head_norm': gain(ks[12], (DEPTH, MLSTM_V_W)),
        'hgrn_lb_logits': normal(ks[13], (DEPTH + 1, HGRN_K_W), 0.1),
        'hgrn_head_norm': gain(ks[14], (DEPTH, HGRN_V_W)),
        'w_proj_m': normal(ks[15], (DEPTH, MLSTM_V_W, D_MODEL), MLSTM_V_W ** -0.5),
        'w_proj_h': normal(ks[16], (DEPTH, HGRN_V_W, D_MODEL), HGRN_V_W ** -0.5),
        'w_out': normal(ks[17], (DEPTH, D_MODEL, D_MODEL), D_MODEL ** -0.5),
        'norm_xattn': gain(ks[18], (DEPTH, D_MODEL)),
        'norm_mem': gain(ks[19], (DEPTH, D_MODEL)),
        'xattn_wq': normal(ks[20], (DEPTH, D_MODEL, D_MODEL), D_MODEL ** -0.5),
        'xattn_wkv': normal(ks[21], (DEPTH, D_MODEL, 2 * D_MODEL), D_MODEL ** -0.5),
        'xattn_wo': normal(ks[22], (DEPTH, D_MODEL, D_MODEL), D_MODEL ** -0.5),
        'norm_ffn2': gain(ks[23], (DEPTH, D_MODEL)),
        'ffn2_w1': normal(ks[24], (DEPTH, D_MODEL, D_FF), D_MODEL ** -0.5),
        'ffn2_w3': normal(ks[25], (DEPTH, D_MODEL, D_FF), D_MODEL ** -0.5),
        'ffn2_w2': normal(ks[26], (DEPTH, D_FF, D_MODEL), D_FF ** -0.5),
        'norm_final': gain(ks[27], (D_MODEL,)),
    }


def reference(x, mem, norm_ffn1, ffn1_w1, ffn1_w3, ffn1_w2, norm_mix, w_in, mlstm_conv_w, mlstm_conv_b,
              mlstm_ig_bias, mlstm_fg_bias, mlstm_head_norm, hgrn_lb_logits, hgrn_head_norm, w_proj_m,
              w_proj_h, w_out, norm_xattn, norm_mem, xattn_wq, xattn_wkv, xattn_wo, norm_ffn2, ffn2_w1,
              ffn2_w3, ffn2_w2, norm_final):
    split_points = []
    acc = 0
    for size in SPLIT_SIZES[:-1]:
        acc += size
        split_points.append(acc)
    lb_all = jnp.cumsum(jax.nn.softmax(hgrn_lb_logits.astype(jnp.float32), axis=0), axis=0)
    for l in range(DEPTH):
        h = rmsnorm(x, norm_ffn1[l])
        x = x + 0.5 * swiglu(h, ffn1_w1[l], ffn1_w3[l], ffn1_w2[l])
        h = rmsnorm(x, norm_mix[l])
        (mq, mk, mv, mo, mi, mf, hq, hf, hi, hg, gate_m, gate_h) = jnp.split(h @ w_in[l], split_points, axis=-1)
        y_m = mlstm_mixer(mq, mk, mv, mo, mi, mf, mlstm_conv_w[l], mlstm_conv_b[l], mlstm_ig_bias[l],
                          mlstm_fg_bias[l], mlstm_head_norm[l]).astype(x.dtype)
        lb = lb_all[l + 1] - lb_all[0]
        y_h = hgrn2_mixer(hq, hf, hi, hg, lb, hgrn_head_norm[l]).astype(x.dtype)
        merged = jax.nn.sigmoid(gate_m) * (y_m @ w_proj_m[l]) + jax.nn.sigmoid(gate_h) * (y_h @ w_proj_h[l])
        x = x + merged @ w_out[l]
        h = rmsnorm(x, norm_xattn[l])
        m = rmsnorm(mem, norm_mem[l])
        x = x + cross_attention(h, m, xattn_wq[l], xattn_wkv[l], xattn_wo[l])
        h = rmsnorm(x, norm_ffn2[l])
        x = x + 0.5 * swiglu(h, ffn2_w1[l], ffn2_w3[l], ffn2_w2[l])
    return rmsnorm(x, norm_final)
```

```python
import math
from contextlib import ExitStack
import numpy as np
import concourse.bass as bass
import concourse.mybir as mybir
from concourse.bass_utils import run_bass_kernel_spmd

F32 = mybir.dt.float32
BF16 = mybir.dt.bfloat16
AF = mybir.ActivationFunctionType
ALU = mybir.AluOpType
AX = mybir.AxisListType

D = 2048
T = 1024
KC = 16
DFF = 5632
NSLAB = 11
EPS = 1e-6
NCORES = 8
DIN = 11272
C_MQ, C_MV, C_MO, C_HQ, C_HF, C_HI, C_HG, C_GM, C_GH = 0, 1024, 2048, 3080, 4104, 5128, 6152, 7176, 9224
LNK = -0.5 * math.log(128.0)
GROUPS = [[0, 1, 2, 3], [4, 5, 6, 7]]


class Reg:
    __slots__ = ("lw", "rd", "excl")

    def __init__(self, excl=False):
        self.lw = None
        self.rd = {}
        self.excl = excl


def regs(*shape):
    if len(shape) == 1:
        return [Reg() for _ in range(shape[0])]
    return [regs(*shape[1:]) for _ in range(shape[0])]


def flat(x):
    if isinstance(x, Reg):
        return [x]
    out = []
    for y in x:
        out.extend(flat(y))
    return out


class Eng:
    def __init__(self, name, h, sem, kind):
        self.name, self.h, self.sem, self.kind = name, h, sem, kind
        self.cnt = 0
        self.seen = {}


class Slot:
    def __init__(self, sem):
        self.sem = sem
        self.cnt = 0
        self.kind = "dma"


class Tracker:
    def _waits(self, eng, reads, writes):
        deps = {}

        def add(src, c):
            if deps.get(src, 0) < c:
                deps[src] = c
        for r in reads:
            if r.lw is not None:
                add(*r.lw)
        for w in writes:
            if w.lw is not None:
                add(*w.lw)
            for s, c in w.rd.items():
                add(s, c)
        for src, c in deps.items():
            if src is eng and eng.kind == "pe":
                continue
            if src is eng and c > eng.cnt:
                raise RuntimeError("self wait on pending")
            if eng.seen.get(src, 0) < c:
                eng.h.wait_ge(src.sem, c)
                eng.seen[src] = c

    def op(self, eng, fn, reads=(), writes=(), inc=True):
        reads, writes = flat(reads), flat(writes)
        ex = [r for r in reads if r.excl and r not in writes]
        if ex:
            writes = writes + ex
            reads = [r for r in reads if not r.excl]
        self._waits(eng, reads, writes)
        ins = fn(eng.h)
        if inc:
            ins.then_inc(eng.sem, 1)
            eng.cnt += 1
            tag = eng.cnt
        else:
            tag = eng.cnt + 1
        for r in reads:
            if r.rd.get(eng, 0) < tag:
                r.rd[eng] = tag
        for w in writes:
            w.lw = (eng, tag)
            w.rd = {}
        return ins

    def dma(self, q, slot, out, in_, reads=(), writes=()):
        reads, writes = flat(reads), flat(writes)
        self._waits(q, reads, writes)
        if slot.cnt and q.seen.get(slot, 0) < slot.cnt:
            q.h.wait_ge(slot.sem, slot.cnt)
            q.seen[slot] = slot.cnt
        q.h.dma_start(out=out, in_=in_).then_inc(slot.sem, 16)
        slot.cnt += 16
        for r in reads:
            r.rd[slot] = slot.cnt
        for w in writes:
            w.lw = (slot, slot.cnt)
            w.rd = {}

    def coll(self, q, slot, fn, reads=(), writes=()):
        reads, writes = flat(reads), flat(writes)
        assert slot.cnt == 0
        self._waits(q, reads, writes)
        fn(q.h).then_inc(slot.sem, 1)
        slot.cnt = 1
        for r in reads:
            r.rd[slot] = 1
        for w in writes:
            w.lw = (slot, 1)
            w.rd = {}


def build_nc(stop="full"):
    nc = bass.Bass("TRN2", target_bir_lowering=False)

    def din(name, shape):
        return nc.dram_tensor(name, list(shape), F32, kind="ExternalInput").ap()

    xT_d = din("xT", [128, KC, T])
    memT_d = din("memT", [128, KC, 256])
    sel_d = din("sel", [128, 8])
    gains_d = din("gains", [128, 8, KC])
    convp_d = din("convp", [128, 8, 5])
    gbias_d = din("gatebias", [128, 8, 8])
    smallp_d = din("smallp", [128, 4, 8])
    consts_d = din("consts", [128, 2, 128])
    wif_d = din("w_if", [128, KC, 8])
    w_in = din("w_in", [D, DIN])
    f1w1, f1w3, f1w2 = din("ffn1_w1", [D, DFF]), din("ffn1_w3", [D, DFF]), din("ffn1_w2", [DFF, D])
    f2w1, f2w3, f2w2 = din("ffn2_w1", [D, DFF]), din("ffn2_w3", [D, DFF]), din("ffn2_w2", [DFF, D])
    wpm, wph = din("w_proj_m", [1024, D]), din("w_proj_h", [1024, D])
    wout, wq, wkv, wo = din("w_out", [D, D]), din("xattn_wq", [D, D]), din("xattn_wkv", [D, 2 * D]), din("xattn_wo", [D, D])
    out_d = nc.dram_tensor("outT", [128, KC, T], F32, kind="ExternalOutput").ap()
    xsp_d = nc.dram_tensor("xspill", [128, KC, T], F32).ap()
    halo_in = nc.dram_tensor("halo_in", [128, 24], F32)
    halo_out = nc.dram_tensor("halo_out", [512, 24], F32)
    mst_in = nc.dram_tensor("mst_in", [128, 1032], F32)
    mst_out = nc.dram_tensor("mst_out", [512, 1032], F32)
    hst_in = [nc.dram_tensor(f"hst_in{i}", [128, 516], F32) for i in range(2)]
    hst_out = [nc.dram_tensor(f"hst_out{i}", [512, 516], F32) for i in range(2)]

    es = ExitStack()
    with es:
        NA = 52992
        arena = es.enter_context(nc.sbuf_tensor("arena", [128, NA], F32))
        aoff = [0]

        def alloc(shape, dt=F32):
            n = 1
            for s in shape:
                n *= s
            words = n if dt == F32 else (n + 1) // 2
            words = (words + 15) // 16 * 16
            o = aoff[0]
            assert o + words <= NA, f"arena overflow {o + words}"
            aoff[0] = o + words
            ap = arena[:, o:o + words]
            if dt != F32:
                ap = ap.bitcast(dt)
            ap = ap[:, 0:n]
            if len(shape) == 2:
                ap = ap.rearrange("p (a b) -> p a b", a=shape[0])
            elif len(shape) == 3:
                ap = ap.rearrange("p (a b c) -> p a b c", a=shape[0], b=shape[1])
            return ap

        def sem(name):
            return es.enter_context(nc.semaphore(name))

        tr = Tracker()
        block = es.enter_context(nc.Block())
        prog = {k: [] for k in ("pe", "act", "dve", "pool", "sp")}

        class H:
            def __init__(self, key):
                self.key = key

            def __getattr__(self, name):
                key = self.key

                def call(*a, **kw):
                    rec = {"name": name, "a": a, "kw": kw, "inc": None}
                    prog[key].append(rec)

                    class R:
                        def then_inc(self_, s, v=1):
                            rec["inc"] = (s, v)
                            return self_
                    return R()
                return call

        PE = Eng("pe", H("pe"), sem("s_pe"), "pe")
        ACT = Eng("act", H("act"), sem("s_act"), "act")
        DVE = Eng("dve", H("dve"), sem("s_dve"), "dve")
        POOL = Eng("pool", H("pool"), sem("s_pool"), "pool")
        SP = Eng("sp", H("sp"), sem("s_sp"), "sp")
        engs = [PE, ACT, DVE, POOL, SP]
        slots = []

        def slot(name):
            s = Slot(sem(name))
            slots.append(s)
            return s

        def barrier(with_pool=False):
            for e in engs:
                if e is POOL and not with_pool:
                    continue
                for s in engs + slots:
                    if s is e or not s.cnt:
                        continue
                    if e.seen.get(s, 0) < s.cnt:
                        e.h.wait_ge(s.sem, s.cnt)
                        e.seen[s] = s.cnt

        def TS(tb):
            return slice(tb * 512, (tb + 1) * 512)

        def CS(j):
            return slice(j * 128, (j + 1) * 128)

        NW = 3
        wb = [alloc([8192], BF16) for _ in range(NW)]
        wr = regs(NW)
        ws = [slot(f"s_w{i}") for i in range(NW)]
        wi = [0]
        psb = [es.enter_context(nc.psum_tensor(f"ps{i}", [128, 512], F32)) for i in range(7)]
        pr = [Reg(excl=True) for _ in range(7)]
        pi = [0]
        pbf = es.enter_context(nc.psum_tensor("pbf", [128, 1024], BF16))
        _pb = Reg(excl=True)
        pbfr = [_pb, _pb]
        tmpf = [alloc([512]) for _ in range(3)]
        tmpr = regs(3)
        ti = [0]
        tmpb = [alloc([512], BF16) for _ in range(3)]
        tmpbr = regs(3)
        tbi = [0]
        rstd = alloc([T])
        r_rstd = regs(2)
        cst = alloc([2, 128])
        tri = cst[:, 0, :]
        ident = cst[:, 1, :]
        identb = alloc([128], BF16)
        ones_bf = alloc([128], BF16)
        ones_f = alloc([128])
        gcols = alloc([8, KC])
        cols = alloc([8])
        selt = alloc([8])
        convp = alloc([8, 5])
        gbias = alloc([8, 8])
        smallp = alloc([4, 8])
        lbv = alloc([2, 8])
        wif = alloc([KC, 8], BF16)
        r_c = Reg()
        s_misc = slot("s_misc")
        s_out = slot("s_out")
        s_outs = [slot(f"s_out{i}") for i in range(8)]
        s_sp2 = slot("s_sp2")
        P0_END = aoff[0]

        def bank():
            i = pi[0]
            pi[0] = (i + 1) % 7
            return psb[i], pr[i]

        def tf():
            i = ti[0]
            ti[0] = (i + 1) % 3
            return tmpf[i], tmpr[i]

        def tbf():
            i = tbi[0]
            tbi[0] = (i + 1) % 3
            return tmpb[i], tmpbr[i]

        def wtile():
            i = wi[0]
            wi[0] = (i + 1) % NW
            return wb[i], wr[i], ws[i]

        tr.dma(SP, s_misc, cst, consts_d[:, :, :], writes=[r_c])
        tr.dma(SP, s_misc, gcols, gains_d[:, :, :], writes=[r_c])
        tr.dma(SP, s_misc, selt, sel_d[:, :], writes=[r_c])
        tr.dma(SP, s_misc, convp, convp_d[:, :, :], writes=[r_c])
        tr.dma(SP, s_misc, gbias, gbias_d[:, :, :], writes=[r_c])
        tr.dma(SP, s_misc, smallp, smallp_d[:, :, :], writes=[r_c])
        s_wif = slot("s_wif")
        tr.dma(POOL, s_wif, wif, wif_d[:, :, :], writes=[r_c])
        tr.op(DVE, lambda h: h.memset(ones_bf, 1.0), reads=[r_c], writes=[r_c])
        tr.op(DVE, lambda h: h.memset(ones_f, 1.0), writes=[r_c])
        tr.op(DVE, lambda h: h.memset(cols[:, 0:1], EPS), writes=[r_c])
        tr.op(DVE, lambda h: h.memset(cols[:, 1:2], 1.0), writes=[r_c])
        tr.op(DVE, lambda h: h.memset(cols[:, 2:3], LNK), writes=[r_c])
        tr.op(DVE, lambda h: h.tensor_copy(identb, ident), reads=[r_c], writes=[r_c])
        tr.op(DVE, lambda h: h.tensor_tensor(lbv[:, 0, :], smallp[:, 3, :], smallp[:, 2, :], ALU.subtract), reads=[r_c], writes=[r_c])
        tr.op(ACT, lambda h: h.activation(out=lbv[:, 1, :], in_=lbv[:, 0, :], func=AF.Sigmoid, scale=-1.0), reads=[r_c], writes=[r_c])
        tr.op(ACT, lambda h: h.activation(out=lbv[:, 0, :], in_=lbv[:, 0, :], func=AF.Sigmoid), reads=[r_c], writes=[r_c])
        eps_c, one_c, lnk_c = cols[:, 0:1], cols[:, 1:2], cols[:, 2:3]

        def rmsnorm(xT, xr, gi, out_fn, nt=T, dim=D):
            ntb = max(1, nt // 512)
            w = min(nt, 512)
            for tb in range(ntb):
                sl = slice(tb * w, (tb + 1) * w)
                ps, psr = bank()
                for kc in range(KC):
                    sq, sqr = tbf()
                    tr.op(ACT, lambda h: h.activation(out=sq[:, 0:w], in_=xT[:, kc, sl], func=AF.Square),
                          reads=[xr[kc][tb]], writes=[sqr])
                    tr.op(PE, lambda h: h.matmul(ps[:, 0:w], ones_bf, sq[:, 0:w], start=(kc == 0), stop=(kc == KC - 1)),
                          reads=[sqr, r_c], writes=[psr], inc=True)
                rt, rtr = tf()
                tr.op(ACT, lambda h: h.activation(out=rt[:, 0:w], in_=ps[:, 0:w], func=AF.Sqrt, bias=eps_c, scale=1.0 / dim),
                      reads=[psr, r_c], writes=[rtr])
                tr.op(DVE, lambda h: h.reciprocal(rstd[:, sl], rt[:, 0:w]), reads=[rtr], writes=[r_rstd[tb]])
                for kc in range(KC):
                    o, oreg = out_fn(kc, tb)
                    tr.op(DVE, lambda h: h.scalar_tensor_tensor(o, xT[:, kc, sl], gcols[:, gi, kc:kc + 1], rstd[:, sl], ALU.mult, ALU.mult),
                          reads=[xr[kc][tb], r_c, r_rstd[tb]], writes=[oreg])

        def load_w(Wd, row0, nk, col0, ncols):
            W, Wr, Ws = wtile()
            Wv = W[:, 0:nk * ncols].rearrange("p (k m) -> p k m", k=nk)
            tr.dma(POOL, Ws, Wv, Wd[row0:row0 + nk * 128, col0:col0 + ncols].rearrange("(k p) m -> p k m", p=128), writes=[Wr])
            return Wv, Wr

        def proj_fm(Wd, row0, nk, col0, ncols, rhs_fn, evac):
            for t0 in range(0, ncols, 512):
                Wv, Wr = load_w(Wd, row0, nk, col0 + t0, 512)
                for j in range(4):
                    for tb in range(2):
                        ps, psr = bank()
                        for kc in range(nk):
                            rhs, rreg = rhs_fn(kc, tb)
                            tr.op(PE, lambda h: h.matmul(ps[:], Wv[:, kc, CS(j)], rhs, start=(kc == 0), stop=(kc == nk - 1)),
                                  reads=[Wr, rreg], writes=[psr], inc=(kc == nk - 1))
                        evac(t0 // 128 + j, tb, ps, psr)

        def proj_tm(Wd, nk, col0, ncols, lhs_fn, evac, ntile=8):
            for t0 in range(0, ncols, 512):
                Wv, Wr = load_w(Wd, 0, nk, col0 + t0, 512)
                for c in range(ntile):
                    ps, psr = bank()
                    for kc in range(nk):
                        lhs, lreg = lhs_fn(kc, c)
                        tr.op(PE, lambda h: h.matmul(ps[:], lhs, Wv[:, kc, :], start=(kc == 0), stop=(kc == nk - 1)),
                              reads=[Wr, lreg], writes=[psr], inc=(kc == nk - 1))
                    evac(c, t0 // 512, ps, psr)

        def ffn(xT, xr, hT, hr, w1, w3, w2):
            m0 = aoff[0]
            Gb = [alloc([4, T], BF16) for _ in range(2)]
            Gr = regs(2, 4, 2)
            for s in range(NSLAB):
                W1v, W1r = load_w(w1, 0, KC, s * 512, 512)
                W3v, W3r = load_w(w3, 0, KC, s * 512, 512)
                W2v, W2r = load_w(w2, s * 512, 4, 0, D)
                G, Gg = Gb[s % 2], Gr[s % 2]
                for j in range(4):
                    for tb in range(2):
                        p1, p1r = bank()
                        for kc in range(KC):
                            tr.op(PE, lambda h: h.matmul(p1[:], W1v[:, kc, CS(j)], hT[:, kc, TS(tb)], start=(kc == 0), stop=(kc == KC - 1)),
                                  reads=[W1r, hr[kc][tb]], writes=[p1r], inc=(kc == KC - 1))
                        p3, p3r = bank()
                        for kc in range(KC):
                            tr.op(PE, lambda h: h.matmul(p3[:], W3v[:, kc, CS(j)], hT[:, kc, TS(tb)], start=(kc == 0), stop=(kc == KC - 1)),
                                  reads=[W3r, hr[kc][tb]], writes=[p3r], inc=(kc == KC - 1))
                        s1, s1r = tf()
                        tr.op(ACT, lambda h: h.activation(out=s1, in_=p1[:], func=AF.Silu), reads=[p1r], writes=[s1r])
                        tr.op(DVE, lambda h: h.tensor_tensor(G[:, j, TS(tb)], s1, p3[:], ALU.mult),
                              reads=[s1r, p3r], writes=[Gg[j][tb]])
                for m in range(KC):
                    for tb in range(2):
                        po, por = bank()
                        for kc in range(4):
                            tr.op(PE, lambda h: h.matmul(po[:], W2v[:, kc, CS(m)], G[:, kc, TS(tb)], start=(kc == 0), stop=(kc == 3)),
                                  reads=[W2r, Gg[kc][tb]], writes=[por], inc=(kc == 3))
                        tr.op(DVE, lambda h: h.scalar_tensor_tensor(xT[:, m, TS(tb)], po[:], 0.5, xT[:, m, TS(tb)], ALU.mult, ALU.add),
                              reads=[por, xr[m][tb]], writes=[xr[m][tb]])
            aoff[0] = m0

        def allgather(i_dram, o_dram, src_ap, src_regs, dst_ap, dst_regs, name, nrank, i_view=None):
            r_i, r_o = Reg(), Reg()
            s1, s2, s3 = slot("s_ci_" + name), slot("s_cc_" + name), slot("s_co_" + name)
            tr.dma(SP, s1, i_dram.ap() if i_view is None else i_view, src_ap, reads=src_regs, writes=[r_i])
            tr.coll(POOL, s2, lambda h: h.collective_compute("AllGather", ALU.bypass, replica_groups=GROUPS,
                                                             ins=[i_dram.ap().opt()], outs=[o_dram.ap().opt()]),
                    reads=[r_i], writes=[r_o])
            tr.dma(SP, s3, dst_ap, o_dram.ap()[0:nrank * 128, :].rearrange("(r p) f -> p r f", p=128), reads=[r_o], writes=dst_regs)

        def finish(xT, xr):
            for tb in range(2):
                for q in range(4):
                    tr.dma(SP, s_outs[tb * 4 + q], out_d[:, 4 * q:4 * q + 4, TS(tb)], xT[:, 4 * q:4 * q + 4, TS(tb)], reads=[xr[m][tb] for m in range(4 * q, 4 * q + 4)])
            for so in s_outs:
                prog["sp"].append({"name": "wait_ge", "a": (so.sem, so.cnt), "kw": {}, "inc": None})

        class Done(Exception):
            pass

        def dump_exit(items):
            barrier()
            outflat = out_d.rearrange("p a b -> p (a b)")
            col = 0
            for ap, n in items:
                tr.dma(POOL, s_out, outflat[:, col:col + n], ap)
                col += n
            prog["pool"].append({"name": "wait_ge", "a": (s_out.sem, s_out.cnt), "kw": {}, "inc": None})
            raise Done()

        def program():
            hT = alloc([KC, T], BF16)
            hr = regs(KC, 2)
            PH_BASE = aoff[0]
            xT = alloc([KC, T])
            xr = regs(KC, 2)
            xs = [slot(f"s_x{i}") for i in range(8)]
            for tb in range(2):
                for q in range(4):
                    tr.dma(SP, xs[tb * 4 + q], xT[:, 4 * q:4 * q + 4, TS(tb)], xT_d[:, 4 * q:4 * q + 4, TS(tb)], writes=[xr[m][tb] for m in range(4 * q, 4 * q + 4)])

            def h_out(kc, tb):
                return hT[:, kc, TS(tb)], hr[kc][tb]

            def x_out_fn(xT, xr):
                return lambda kc, tb: (xT[:, kc, TS(tb)], xr[kc][tb])

            def h_rhs(kc, tb):
                return hT[:, kc, TS(tb)], hr[kc][tb]

            rmsnorm(xT, xr, 0, h_out)
            ffn(xT, xr, hT, hr, f1w1, f1w3, f1w2)
            if stop == "ffn1":
                finish(xT, xr)
            else:
                rmsnorm(xT, xr, 1, h_out)
                r_xsp = Reg()
                s_spill = slot("s_spill")
                for q in range(4):
                    tr.dma(SP, s_spill, xsp_d[:, 4 * q:4 * q + 4, :], xT[:, 4 * q:4 * q + 4, :], reads=[xr[m] for m in range(4 * q, 4 * q + 4)], writes=[r_xsp])
                barrier()
                aoff[0] = PH_BASE

                ymT = alloc([8, T], BF16)
                r_ym = regs(8, 8)
                MIX_M = aoff[0]

                qkT = alloc([8, T], BF16)
                r_qk = regs(8)
                ktm = alloc([8, 4, 128], BF16)
                r_ktm = regs(4, 2)
                av = alloc([8, 4, 258], BF16)
                r_av = regs(8, 4)
                r_avd = Reg()
                ifr = alloc([8, 8])
                zb = alloc([8, 8])
                ex = alloc([8, 4])
                lt = alloc([8, 4])
                t2 = alloc([8, 4])
                alpha = alloc([8, 4])
                beta = alloc([8, 4])
                ebl = alloc([8, 4])
                totl = alloc([8, 4])
                r_g = regs(10)
                pay = alloc([1032])
                r_pay = regs(5)
                mA = aoff[0]
                qkraw = alloc([8, T + 3])
                r_raw = regs(8, 2)
                r_halo = Reg()
                hgat = alloc([4, 24])
                r_hgat = Reg()
                acc = [alloc([T]) for _ in range(2)]
                r_acc = regs(2)

                def ev_raw(m, tb, ps, psr):
                    tr.op(ACT, lambda h: h.activation(out=qkraw[:, m, 3 + tb * 512:3 + (tb + 1) * 512], in_=ps[:], func=AF.Copy),
                          reads=[psr], writes=[r_raw[m][tb]])
                proj_fm(w_in, 0, KC, C_MQ, 1024, h_rhs, ev_raw)
                allgather(halo_in, halo_out, qkraw[:, :, T:T + 3], [r_raw[m][1] for m in range(8)], hgat, [r_hgat], "halo", 4,
                          i_view=halo_in.ap().rearrange("p (a b) -> p a b", a=8))
                psg, psgr = bank()
                for c in range(8):
                    for kc in range(KC):
                        tr.op(PE, lambda h: h.matmul(psg[:, c * 8:(c + 1) * 8], hT[:, kc, CS(c)], wif[:, kc, :], start=(kc == 0), stop=(kc == KC - 1)),
                              reads=[hr[kc][c // 4], r_c], writes=[psgr], inc=(kc == KC - 1 and c == 7))
                tr.op(ACT, lambda h: h.activation(out=ifr, in_=psg[:, 0:64].rearrange("p (a b) -> p a b", a=8), func=AF.Copy), reads=[psgr], writes=[r_g[0]])
                tr.op(DVE, lambda h: h.tensor_tensor(zb, ifr, gbias, ALU.add), reads=[r_g[0], r_c], writes=[r_g[1]])
                tr.op(ACT, lambda h: h.activation(out=ex, in_=zb[:, :, 4:8], func=AF.Exp, scale=-1.0), reads=[r_g[1]], writes=[r_g[2]])
                tr.op(ACT, lambda h: h.activation(out=lt, in_=ex, func=AF.Ln, bias=one_c), reads=[r_g[2], r_c], writes=[r_g[3]])
                psc, pscr = bank()
                pst, pstr = bank()
                for c in range(8):
                    tr.op(PE, lambda h: h.matmul(psc[:, c * 4:(c + 1) * 4], tri, lt[:, c, :], start=True, stop=True), reads=[r_g[3], r_c], writes=[pscr], inc=(c == 7))
                for c in range(8):
                    tr.op(PE, lambda h: h.matmul(pst[:, c * 4:(c + 1) * 4], ones_f, lt[:, c, :], start=True, stop=True), reads=[r_g[3], r_c], writes=[pstr], inc=(c == 7))
                pscv = psc[:, 0:32].rearrange("p (a b) -> p a b", a=8)
                pstv = pst[:, 0:32].rearrange("p (a b) -> p a b", a=8)
                tr.op(DVE, lambda h: h.tensor_tensor(t2, zb[:, :, 0:4], pscv, ALU.add), reads=[r_g[1], pscr], writes=[r_g[4]])
                tr.op(ACT, lambda h: h.activation(out=alpha, in_=t2, func=AF.Exp, bias=lnk_c), reads=[r_g[4], r_c], writes=[r_g[5]])
                tr.op(ACT, lambda h: h.activation(out=beta, in_=pscv, func=AF.Exp, scale=-1.0), reads=[pscr], writes=[r_g[6]])
                tr.op(ACT, lambda h: h.activation(out=ebl, in_=pstv, func=AF.Exp, scale=-1.0), reads=[pstr], writes=[r_g[7]])
                tr.op(ACT, lambda h: h.activation(out=totl, in_=pstv, func=AF.Copy), reads=[pstr], writes=[r_g[8]])
                tr.op(DVE, lambda h: h.tensor_reduce(pay[:, 1028:1032], totl.rearrange("p c h -> p h c"), AX.X, ALU.add), reads=[r_g[8]], writes=[r_pay[4]])
                if stop == "m1c":
                    dump_exit([(alpha.rearrange("p a b -> p (a b)"), 32), (beta.rearrange("p a b -> p (a b)"), 32), (ebl.rearrange("p a b -> p (a b)"), 32), (pay[:, 1028:1032], 4)])
                tr.op(DVE, lambda h: h.tensor_copy(av[:, :, :, 256], alpha), reads=[r_g[5]], writes=[r_avd])

                def ev_v(c, blk, ps, psr):
                    for hh in range(2):
                        hd = 2 * blk + hh
                        tr.op(DVE, lambda h: h.tensor_scalar(av[:, c, hd, 0:256], ps[:, hh * 256:(hh + 1) * 256], alpha[:, c, hd:hd + 1], None, ALU.mult),
                              reads=[psr, r_g[5]], writes=[r_av[c][hd]])
                proj_tm(w_in, KC, C_MV, 1024, lambda kc, c: (hT[:, kc, CS(c)], hr[kc][c // 4]), ev_v)


                if stop == "m1d":
                    dump_exit([(av.rearrange("p a b c -> p (a b c)"), 8256)])

                halo_dst = qkraw[:, :, 0:3]
                for j in range(4):
                    src = hgat[:, j, :].rearrange("p (a b) -> p a b", a=8)
                    if j == 0:
                        tr.op(DVE, lambda h: h.tensor_scalar(halo_dst, src, selt[:, 0:1], None, ALU.mult), reads=[r_hgat, r_c], writes=[r_halo])
                    else:
                        tr.op(DVE, lambda h: h.scalar_tensor_tensor(halo_dst, src, selt[:, j:j + 1], halo_dst, ALU.mult, ALU.add),
                              reads=[r_hgat, r_c, r_halo], writes=[r_halo])
                for m in range(8):
                    a, ar = acc[m % 2], r_acc[m % 2]
                    tr.op(DVE, lambda h: h.tensor_scalar(a, qkraw[:, m, 0:T], convp[:, m, 0:1], convp[:, m, 4:5], ALU.mult, ALU.add),
                          reads=[r_raw[m], r_halo, r_c], writes=[ar])
                    for j in range(1, 4):
                        tr.op(DVE, lambda h: h.scalar_tensor_tensor(a, qkraw[:, m, j:j + T], convp[:, m, j:j + 1], a, ALU.mult, ALU.add),
                              reads=[r_raw[m], r_halo, r_c, ar], writes=[ar])
                    tr.op(ACT, lambda h: h.activation(out=qkT[:, m, :], in_=a, func=AF.Silu), reads=[ar], writes=[r_qk[m]])
                if stop == "m1":
                    dump_exit([(qkT.rearrange("p a b -> p (a b)"), 8192)])
                for hd in range(4):
                    for half in range(2):
                        for cc in range(4):
                            c = half * 4 + cc
                            tr.op(PE, lambda h: h.transpose(pbf[:, half * 512 + cc * 128:half * 512 + (cc + 1) * 128], qkT[:, 4 + hd, CS(c)], identb),
                                  reads=[r_qk[4 + hd], r_c], writes=[pbfr[half]], inc=(cc == 3))
                        tr.op(ACT, lambda h: h.activation(out=ktm[:, half * 4:half * 4 + 4, hd, :],
                                                          in_=pbf[:, half * 512:(half + 1) * 512].rearrange("p (a b) -> p a b", a=4), func=AF.Copy),
                              reads=[pbfr[half]], writes=[r_ktm[hd][half]])
                if stop == "m1b":
                    dump_exit([(ktm.rearrange("p a b c -> p (a b c)"), 4096)])
                barrier()
                aoff[0] = mA
                Dbf = alloc([4, 8, 258], BF16)
                r_dbf = regs(4, 8)
                Dst = alloc([1028])
                mB = aoff[0]
                gat = alloc([3, 1032])
                r_gat = Reg()
                eB = alloc([3, 4])
                Rb = alloc([1028])
                r_R, r_Dst = Reg(), regs(4)
                def m_chain(Dv, Dr, write_bf):
                    for c in range(8):
                        for hd in range(4):
                            dv = Dv[:, hd * 257:(hd + 1) * 257]
                            if write_bf:
                                tr.op(ACT, lambda h: h.activation(out=Dbf[:, hd, c, 0:257], in_=dv, func=AF.Copy), reads=[Dr[hd]], writes=[r_dbf[hd][c]])
                                if c == 7:
                                    continue
                            pu, pur = bank()
                            tr.op(PE, lambda h: h.matmul(pu[:, 0:257], ktm[:, c, hd, :], av[:, c, hd, 0:257], start=True, stop=True),
                                  reads=[r_ktm[hd][c // 4], r_av[c][hd], r_avd], writes=[pur])
                            tr.op(DVE, lambda h: h.tensor_tensor(dv, pu[:, 0:257], dv, ALU.add), reads=[pur, Dr[hd]], writes=[Dr[hd]])
                            tr.op(DVE, lambda h: h.tensor_scalar(dv, dv, ebl[:, c, hd:hd + 1], None, ALU.mult), reads=[Dr[hd], r_g[7]], writes=[Dr[hd]])

                tr.op(DVE, lambda h: h.memset(pay[:, 0:1028], 0.0), writes=r_pay[0:4])
                m_chain(pay, r_pay, False)
                if stop == "m1e":
                    dump_exit([(pay, 1032)])
                allgather(mst_in, mst_out, pay, r_pay, gat, [r_gat], "mst", 3)
                Wog = [load_w(w_in, 0, KC, C_MO + t * 512, 512) for t in range(2)]

                def og_part(tb, groups):
                    for g in groups:
                        t, j = g // 4, g % 4
                        Wv, Wr = Wog[t]
                        ps, psr = bank()
                        for kc in range(KC):
                            tr.op(PE, lambda h: h.matmul(ps[:], Wv[:, kc, CS(j)], hT[:, kc, TS(tb)], start=(kc == 0), stop=(kc == KC - 1)),
                                  reads=[Wr, hr[kc][tb]], writes=[psr], inc=(kc == KC - 1))
                        tr.op(ACT, lambda h: h.activation(out=ymT[:, g, TS(tb)], in_=ps[:], func=AF.Sigmoid), reads=[psr], writes=r_ym[g][tb * 4:tb * 4 + 4])
                og_part(0, list(range(8)))
                tr.op(ACT, lambda h: h.activation(out=eB, in_=gat[:, :, 1028:1032], func=AF.Exp, scale=-1.0), reads=[r_gat], writes=[r_R])
                tr.op(DVE, lambda h: h.tensor_scalar(Dst, gat[:, 0, 0:1028], selt[:, 4:5], None, ALU.mult), reads=[r_gat, r_c], writes=r_Dst)
                for j in (1, 2):
                    prev = gat[:, 0, 0:1028] if j == 1 else Rb
                    for hd in range(4):
                        sl = slice(hd * 257, (hd + 1) * 257)
                        tr.op(DVE, lambda h: h.scalar_tensor_tensor(Rb[:, sl], prev[:, sl], eB[:, j, hd:hd + 1], gat[:, j, sl], ALU.mult, ALU.add),
                              reads=[r_gat, r_R], writes=[r_R])
                    tr.op(DVE, lambda h: h.scalar_tensor_tensor(Dst, Rb, selt[:, 4 + j:5 + j], Dst, ALU.mult, ALU.add), reads=[r_R, r_c] + r_Dst, writes=r_Dst)
                if stop == "m1f":
                    dump_exit([(Dst, 1028)])
                barrier()
                aoff[0] = mB
                m_chain(Dst, r_Dst, True)
                if stop == "m1g":
                    dump_exit([(Dbf.rearrange("p a b c -> p (a b c)"), 8256)])
                nb = [alloc([4, 257]) for _ in range(2)]
                hn = [alloc([4, 256]) for _ in range(2)]
                sm = [alloc([8, 4]) for _ in range(2)]
                PTb = [alloc([4, 128], BF16) for _ in range(2)]
                junk = alloc([256])
                r_nb, r_hn, r_sm, r_pt, r_junk = regs(2, 4), regs(2, 4), regs(2), regs(2), Reg()
                for c in range(8):
                    b = c % 2
                    pS, pSr = bank()
                    for hd in range(4):
                        tr.op(PE, lambda h: h.matmul(pS[:, CS(hd)], qkT[:, 4 + hd, CS(c)], qkT[:, hd, CS(c)], start=True, stop=True),
                              reads=[r_qk[4 + hd], r_qk[hd]], writes=[pSr], inc=(hd == 3))
                    tr.op(DVE, lambda h: h.tensor_tensor(PTb[b], pS[:].rearrange("p (a b) -> p a b", a=4), tri.unsqueeze(1).to_broadcast([128, 4, 128]), ALU.mult),
                          reads=[pSr, r_c], writes=[r_pt[b]])
                    pN = []
                    for hd in range(4):
                        p, prr = bank()
                        pN.append((p, prr))
                        tr.op(PE, lambda h: h.matmul(p[:, 0:257], PTb[b][:, hd, :], av[:, c, hd, 0:257], start=True, stop=False),
                              reads=[r_pt[b], r_av[c][hd], r_avd], writes=[prr], inc=False)
                        tr.op(PE, lambda h: h.matmul(p[:, 0:257], qkT[:, hd, CS(c)], Dbf[:, hd, c, 0:257], start=False, stop=True),
                              reads=[r_qk[hd], r_dbf[hd][c]], writes=[prr], inc=True)
                    smb = sm[b]
                    tr.op(DVE, lambda h: h.memset(smb[:, 0, :], 0.0), writes=[r_sm[b]])
                    for hd in range(4):
                        p, prr = pN[hd]
                        tr.op(DVE, lambda h: h.tensor_scalar(nb[b][:, hd, :], p[:, 0:257], beta[:, c, hd:hd + 1], None, ALU.mult),
                              reads=[prr, r_g[6]], writes=[r_nb[b][hd]])
                    for hd in range(4):
                        tr.op(ACT, lambda h: h.activation(out=junk, in_=nb[b][:, hd, 0:256], func=AF.Square, accum_out=smb[:, 0, hd:hd + 1]),
                              reads=[r_nb[b][hd], r_sm[b]], writes=[r_sm[b], r_junk])
                    den = nb[b][:, :, 256]
                    tr.op(DVE, lambda h: h.tensor_scalar(smb[:, 1, :], den, -1.0, None, ALU.mult), reads=r_nb[b], writes=[r_sm[b]])
                    tr.op(DVE, lambda h: h.scalar_tensor_tensor(smb[:, 2, :], smb[:, 1, :], 1.0, den, ALU.max, ALU.max), reads=r_nb[b] + [r_sm[b]], writes=[r_sm[b]])
                    tr.op(DVE, lambda h: h.reciprocal(smb[:, 3, :], smb[:, 2, :]), reads=[r_sm[b]], writes=[r_sm[b]])
                    tr.op(DVE, lambda h: h.tensor_tensor(smb[:, 4, :], smb[:, 0, :], smb[:, 3, :], ALU.mult), reads=[r_sm[b]], writes=[r_sm[b]])
                    tr.op(DVE, lambda h: h.tensor_tensor(smb[:, 4, :], smb[:, 4, :], smb[:, 3, :], ALU.mult), reads=[r_sm[b]], writes=[r_sm[b]])
                    tr.op(ACT, lambda h: h.activation(out=smb[:, 5, :], in_=smb[:, 4, :], func=AF.Sqrt, bias=eps_c, scale=1.0 / 256), reads=[r_sm[b], r_c], writes=[r_sm[b]])
                    tr.op(DVE, lambda h: h.reciprocal(smb[:, 6, :], smb[:, 5, :]), reads=[r_sm[b]], writes=[r_sm[b]])
                    tr.op(DVE, lambda h: h.tensor_tensor(smb[:, 7, :], smb[:, 6, :], smb[:, 3, :], ALU.mult), reads=[r_sm[b]], writes=[r_sm[b]])
                    for hd in range(4):
                        tr.op(DVE, lambda h: h.tensor_scalar(hn[b][:, hd, :], nb[b][:, hd, 0:256], smb[:, 7, hd:hd + 1], None, ALU.mult),
                              reads=[r_nb[b][hd], r_sm[b]], writes=[r_hn[b][hd]])
                    for hp in range(2):
                        pT, pTr = bank()
                        for i in range(4):
                            hd, eh = 2 * hp + i // 2, i % 2
                            tr.op(PE, lambda h: h.transpose(pT[:, CS(i)], hn[b][:, hd, CS(eh)], ident), reads=[r_hn[b][hd], r_c], writes=[pTr], inc=(i == 3))
                        for i in range(4):
                            hd, eh = 2 * hp + i // 2, i % 2
                            fe = hd * 2 + eh
                            tr.op(DVE, lambda h: h.scalar_tensor_tensor(ymT[:, fe, CS(c)], pT[:, CS(i)], smallp[:, 0, fe:fe + 1], ymT[:, fe, CS(c)], ALU.mult, ALU.mult),
                                  reads=[pTr, r_c, r_ym[fe][c]], writes=[r_ym[fe][c]])
                        if c < 4:
                            og_part(1, [2 * c + hp])

                if stop == "m2":
                    dump_exit([(ymT.rearrange("p a b -> p (a b)"), 8192)])
                barrier()
                aoff[0] = MIX_M
                yhT = alloc([8, T], BF16)
                r_yh = regs(8, 2)
                MIX_BASE = aoff[0]

                for half in range(2):
                    qt = alloc([4, T], BF16)
                    kt = alloc([4, T], BF16)
                    ktm2 = alloc([8, 4, 128], BF16)
                    vtm = alloc([8, 512], BF16)
                    egl = alloc([4, 8])
                    payh = alloc([516])
                    r_qt, r_kt, r_ktm2, r_vtm, r_egl, r_payh = regs(4, 2), regs(4), regs(4, 2), regs(8, 4), regs(4), regs(5)
                    hA = aoff[0]
                    sg = alloc([4, T])
                    gb_ = alloc([T])
                    Gb_ = alloc([T])
                    e1 = alloc([T])
                    r_sg, r_gb, r_Gb, r_e1 = regs(4, 2), Reg(), Reg(), Reg()

                    def ev_q(m, tb, ps, psr):
                        tr.op(ACT, lambda h: h.activation(out=qt[:, m, TS(tb)], in_=ps[:], func=AF.Silu), reads=[psr], writes=[r_qt[m][tb]])
                    proj_fm(w_in, 0, KC, C_HQ + half * 512, 512, h_rhs, ev_q)

                    def ev_sg(m, tb, ps, psr):
                        tr.op(ACT, lambda h: h.activation(out=sg[:, m, TS(tb)], in_=ps[:], func=AF.Sigmoid), reads=[psr], writes=[r_sg[m][tb]])
                    proj_fm(w_in, 0, KC, C_HF + half * 512, 512, h_rhs, ev_sg)
                    Whi, Whir = load_w(w_in, 0, KC, C_HI + half * 512, 512)

                    def hi_tiles(cs):
                        for c in cs:
                            ps, psr = bank()
                            for kc in range(KC):
                                tr.op(PE, lambda h: h.matmul(ps[:], hT[:, kc, CS(c)], Whi[:, kc, :], start=(kc == 0), stop=(kc == KC - 1)),
                                      reads=[Whir, hr[kc][c // 4]], writes=[psr], inc=(kc == KC - 1))
                            tr.op(ACT, lambda h: h.activation(out=vtm[:, c, :], in_=ps[:], func=AF.Copy), reads=[psr], writes=r_vtm[c])
                    for i in range(4):
                        gh = half * 4 + i
                        f_ = sg[:, i, :]
                        tr.op(DVE, lambda h: h.tensor_scalar(f_, f_, lbv[:, 1, gh:gh + 1], lbv[:, 0, gh:gh + 1], ALU.mult, ALU.add), reads=r_sg[i] + [r_c], writes=r_sg[i])
                        tr.op(ACT, lambda h: h.activation(out=gb_, in_=f_, func=AF.Ln), reads=r_sg[i], writes=[r_gb])
                        for c in range(8):
                            tr.op(DVE, lambda h: h.tensor_tensor_scan(Gb_[:, CS(c)], ones_f, gb_[:, CS(c)], 0.0, ALU.mult, ALU.add), reads=[r_gb, r_c], writes=[r_Gb])
                        tr.op(DVE, lambda h: h.tensor_reduce(payh[:, 512 + i:513 + i], gb_, AX.X, ALU.add), reads=[r_gb], writes=[r_payh[4]])
                        tr.op(DVE, lambda h: h.tensor_scalar(f_, f_, -1.0, 1.0, ALU.mult, ALU.add), reads=r_sg[i], writes=r_sg[i])
                        tr.op(ACT, lambda h: h.activation(out=e1, in_=Gb_, func=AF.Exp, scale=-1.0), reads=[r_Gb], writes=[r_e1])
                        tr.op(DVE, lambda h: h.tensor_tensor(kt[:, i, :], f_, e1, ALU.mult), reads=r_sg[i] + [r_e1], writes=[r_kt[i]])
                        tr.op(ACT, lambda h: h.activation(out=egl[:, i, :], in_=Gb_.rearrange("p (c t) -> p c t", t=128)[:, :, 127], func=AF.Exp), reads=[r_Gb], writes=[r_egl[i]])
                        tr.op(ACT, lambda h: h.activation(out=e1, in_=Gb_, func=AF.Exp, bias=lnk_c), reads=[r_Gb, r_c, r_kt[i]], writes=[r_e1])
                        tr.op(DVE, lambda h: h.tensor_tensor(qt[:, i, :], qt[:, i, :], e1, ALU.mult), reads=r_qt[i] + [r_e1], writes=r_qt[i])
                        hi_tiles([2 * i, 2 * i + 1])
                    for i in range(4):
                        for hf in range(2):
                            for cc in range(4):
                                c = hf * 4 + cc
                                tr.op(PE, lambda h: h.transpose(pbf[:, hf * 512 + cc * 128:hf * 512 + (cc + 1) * 128], kt[:, i, CS(c)], identb),
                                      reads=[r_kt[i], r_c], writes=[pbfr[hf]], inc=(cc == 3))
                            tr.op(ACT, lambda h: h.activation(out=ktm2[:, hf * 4:hf * 4 + 4, i, :],
                                                              in_=pbf[:, hf * 512:(hf + 1) * 512].rearrange("p (a b) -> p a b", a=4), func=AF.Copy),
                                  reads=[pbfr[hf]], writes=[r_ktm2[i][hf]])
                    barrier()
                    aoff[0] = hA
                    oT = alloc([4, T])
                    r_oT = regs(4, 8)
                    Sbf = alloc([4, 8, 128], BF16)
                    r_sbf = regs(4, 8)
                    gath = alloc([3, 516])
                    r_gath = Reg()
                    eBh = alloc([3, 4])
                    Rh = alloc([512])
                    Sst = alloc([512])
                    r_Rh, r_Sst = Reg(), regs(4)
                    Pm = [alloc([4, 128], BF16) for _ in range(2)]
                    r_pm = regs(2)

                    def h_chain(Sv, Sr, write_bf):
                        for c in range(8):
                            if write_bf:
                                for i in range(4):
                                    tr.op(ACT, lambda h: h.activation(out=Sbf[:, i, c, :], in_=Sv[:, CS(i)], func=AF.Copy), reads=[Sr[i]], writes=[r_sbf[i][c]])
                                if c == 7:
                                    continue
                            pu, pur = bank()
                            for i in range(4):
                                tr.op(PE, lambda h: h.matmul(pu[:, CS(i)], ktm2[:, c, i, :], vtm[:, c, CS(i)], start=True, stop=True),
                                      reads=[r_ktm2[i][c // 4], r_vtm[c][i]], writes=[pur], inc=(i == 3))
                            for i in range(4):
                                sv = Sv[:, CS(i)]
                                tr.op(DVE, lambda h: h.tensor_tensor(sv, pu[:, CS(i)], sv, ALU.add), reads=[pur, Sr[i]], writes=[Sr[i]])
                                tr.op(DVE, lambda h: h.tensor_scalar(sv, sv, egl[:, i, c:c + 1], None, ALU.mult), reads=[Sr[i], r_egl[i]], writes=[Sr[i]])

                    tr.op(DVE, lambda h: h.memset(payh[:, 0:512], 0.0), writes=r_payh[0:4])
                    h_chain(payh, r_payh, False)
                    allgather(hst_in[half], hst_out[half], payh, r_payh, gath, [r_gath], f"hst{half}", 3)

                    def ev_g(m, tb, ps, psr):
                        tr.op(ACT, lambda h: h.activation(out=yhT[:, half * 4 + m, TS(tb)], in_=ps[:], func=AF.Silu), reads=[psr], writes=[r_yh[half * 4 + m][tb]])
                    proj_fm(w_in, 0, KC, C_HG + half * 512, 512, h_rhs, ev_g)
                    tr.op(ACT, lambda h: h.activation(out=eBh, in_=gath[:, :, 512:516], func=AF.Exp), reads=[r_gath], writes=[r_Rh])
                    tr.op(DVE, lambda h: h.tensor_scalar(Sst, gath[:, 0, 0:512], selt[:, 4:5], None, ALU.mult), reads=[r_gath, r_c], writes=r_Sst)
                    for j in (1, 2):
                        prev = gath[:, 0, 0:512] if j == 1 else Rh
                        for i in range(4):
                            tr.op(DVE, lambda h: h.scalar_tensor_tensor(Rh[:, CS(i)], prev[:, CS(i)], eBh[:, j, i:i + 1], gath[:, j, CS(i)], ALU.mult, ALU.add),
                                  reads=[r_gath, r_Rh], writes=[r_Rh])
                        tr.op(DVE, lambda h: h.scalar_tensor_tensor(Sst, Rh, selt[:, 4 + j:5 + j], Sst, ALU.mult, ALU.add), reads=[r_Rh, r_c] + r_Sst, writes=r_Sst)
                    h_chain(Sst, r_Sst, True)
                    for c in range(8):
                        b = c % 2
                        pA, pAr = bank()
                        for i in range(4):
                            tr.op(PE, lambda h: h.matmul(pA[:, CS(i)], kt[:, i, CS(c)], qt[:, i, CS(c)], start=True, stop=True),
                                  reads=[r_kt[i], r_qt[i][c // 4]], writes=[pAr], inc=(i == 3))
                        tr.op(DVE, lambda h: h.tensor_tensor(Pm[b], pA[:].rearrange("p (a b) -> p a b", a=4), tri.unsqueeze(1).to_broadcast([128, 4, 128]), ALU.mult),
                              reads=[pAr, r_c], writes=[r_pm[b]])
                        pO, pOr = bank()
                        for i in range(4):
                            tr.op(PE, lambda h: h.matmul(pO[:, CS(i)], vtm[:, c, CS(i)], Pm[b][:, i, :], start=True, stop=False),
                                  reads=[r_vtm[c][i], r_pm[b]], writes=[pOr], inc=False)
                            tr.op(PE, lambda h: h.matmul(pO[:, CS(i)], Sbf[:, i, c, :], qt[:, i, CS(c)], start=False, stop=True),
                                  reads=[r_sbf[i][c], r_qt[i][c // 4]], writes=[pOr], inc=(i == 3))
                        tr.op(ACT, lambda h: h.activation(out=oT[:, :, CS(c)], in_=pO[:].rearrange("p (a b) -> p a b", a=4), func=AF.Copy),
                              reads=[pOr], writes=[r_oT[i][c] for i in range(4)])
                    for i in range(4):
                        gh = half * 4 + i
                        for tb in range(2):
                            sq, sqr = tbf()
                            tr.op(ACT, lambda h: h.activation(out=sq, in_=oT[:, i, TS(tb)], func=AF.Square), reads=r_oT[i][tb * 4:tb * 4 + 4], writes=[sqr])
                            ps, psr = bank()
                            tr.op(PE, lambda h: h.matmul(ps[:], ones_bf, sq, start=True, stop=True), reads=[sqr, r_c], writes=[psr])
                            rt, rtr = tf()
                            tr.op(ACT, lambda h: h.activation(out=rt, in_=ps[:], func=AF.Sqrt, bias=eps_c, scale=1.0 / 128), reads=[psr, r_c], writes=[rtr])
                            tr.op(DVE, lambda h: h.reciprocal(rt, rt), reads=[rtr], writes=[rtr])
                            y1, y1r = tf()
                            tr.op(DVE, lambda h: h.scalar_tensor_tensor(y1, oT[:, i, TS(tb)], smallp[:, 1, gh:gh + 1], rt, ALU.mult, ALU.mult),
                                  reads=r_oT[i][tb * 4:tb * 4 + 4] + [r_c, rtr], writes=[y1r])
                            tr.op(DVE, lambda h: h.tensor_tensor(yhT[:, gh, TS(tb)], y1, yhT[:, gh, TS(tb)], ALU.mult), reads=[y1r, r_yh[gh][tb]], writes=[r_yh[gh][tb]])
                    barrier()
                    aoff[0] = MIX_BASE

                if stop == "h":
                    dump_exit([(ymT.rearrange("p a b -> p (a b)"), 8192), (yhT.rearrange("p a b -> p (a b)"), 8192)])
                aoff[0] = NA - 8192 - 16
                mgT = alloc([KC, T], BF16)
                r_mg = regs(KC, 2)
                aoff[0] = MIX_BASE
                sgb = [alloc([4, T], BF16) for _ in range(2)]
                r_sgb = regs(2, 4, 2)
                rA, rB, rC = wr[0], wr[1], [wr[2], wr[2]]
                steps = [(wpm, C_GM, blk) for blk in range(4)] + [(wph, C_GH, blk) for blk in range(4)]
                Wg_v, Wp_v = {}, {}

                def mg_loads(st):
                    srcp, cg, blk = steps[st]
                    gb, gr, gs = (wb[0], rA, ws[0]) if st % 2 == 0 else (wb[1], rB, ws[1])
                    Wg = gb[:, 0:8192].rearrange("p (k m) -> p k m", k=KC)
                    tr.dma(POOL, gs, Wg, w_in[0:D, cg + blk * 512:cg + (blk + 1) * 512].rearrange("(k p) m -> p k m", p=128), writes=[gr])
                    ph = st % 2
                    Wp = wb[2][:, ph * 4096:(ph + 1) * 4096].rearrange("p (k m) -> p k m", k=8)
                    tr.dma(POOL, ws[2], Wp, srcp[0:1024, blk * 512:(blk + 1) * 512].rearrange("(k p) m -> p k m", p=128), writes=[rC[ph]])
                    Wg_v[st], Wp_v[st] = (Wg, gr), (Wp, rC[ph])

                mg_loads(0)
                for st in range(8):
                    if st + 1 < 8:
                        mg_loads(st + 1)
                    (Wg, Wgr), (Wp, Wpr) = Wg_v[st], Wp_v[st]
                    blk = steps[st][2]
                    second = st >= 4
                    sg_, sgr_ = sgb[st % 2], r_sgb[st % 2]
                    for j in range(4):
                        for tb in range(2):
                            p2, p2r = bank()
                            for kc in range(KC):
                                tr.op(PE, lambda h: h.matmul(p2[:], Wg[:, kc, CS(j)], hT[:, kc, TS(tb)], start=(kc == 0), stop=(kc == KC - 1)),
                                      reads=[Wgr, hr[kc][tb]], writes=[p2r], inc=(kc == KC - 1))
                            tr.op(ACT, lambda h: h.activation(out=sg_[:, j, TS(tb)], in_=p2[:], func=AF.Sigmoid), reads=[p2r], writes=[sgr_[j][tb]])
                    for j in range(4):
                        m = blk * 4 + j
                        for tb in range(2):
                            p1, p1r = bank()
                            for kc in range(8):
                                if second:
                                    rhs, rr = yhT[:, kc, TS(tb)], [r_yh[kc][tb]]
                                else:
                                    rhs, rr = ymT[:, kc, TS(tb)], r_ym[kc][tb * 4:tb * 4 + 4]
                                tr.op(PE, lambda h: h.matmul(p1[:], Wp[:, kc, CS(j)], rhs, start=(kc == 0), stop=(kc == 7)),
                                      reads=[Wpr] + rr, writes=[p1r], inc=(kc == 7))
                            if not second:
                                tr.op(DVE, lambda h: h.tensor_tensor(mgT[:, m, TS(tb)], sg_[:, j, TS(tb)], p1[:], ALU.mult), reads=[sgr_[j][tb], p1r], writes=[r_mg[m][tb]])
                            else:
                                s2, s2r = tf()
                                tr.op(DVE, lambda h: h.tensor_tensor(s2, sg_[:, j, TS(tb)], p1[:], ALU.mult), reads=[sgr_[j][tb], p1r], writes=[s2r])
                                tr.op(DVE, lambda h: h.tensor_tensor(mgT[:, m, TS(tb)], s2, mgT[:, m, TS(tb)], ALU.add), reads=[s2r, r_mg[m][tb]], writes=[r_mg[m][tb]])
                barrier()

                aoff[0] = P0_END
                xT = alloc([KC, T])
                xr = regs(KC, 2)
                for q in range(4):
                    tr.dma(SP, xs[q], xT[:, 4 * q:4 * q + 4, :], xsp_d[:, 4 * q:4 * q + 4, :], reads=[r_xsp], writes=[xr[m] for m in range(4 * q, 4 * q + 4)])

                def ev_addx(m, tb, ps, psr):
                    tr.op(DVE, lambda h: h.tensor_tensor(xT[:, m, TS(tb)], ps[:], xT[:, m, TS(tb)], ALU.add), reads=[psr, xr[m][tb]], writes=[xr[m][tb]])
                proj_fm(wout, 0, KC, 0, D, lambda kc, tb: (mgT[:, kc, TS(tb)], r_mg[kc][tb]), ev_addx)
                barrier()
                if stop == "mix":
                    finish(xT, xr)
                else:
                    hT = alloc([KC, T], BF16)
                    hr = regs(KC, 2)
                    X_BASE = aoff[0]
                    KT = alloc([KC, 256], BF16)
                    Vx = alloc([2, D], BF16)
                    r_KT, r_Vx = regs(KC), regs(2, 4)
                    xA = aoff[0]
                    memT = alloc([KC, 256])
                    mnT = alloc([KC, 256], BF16)
                    r_mem, r_mn = regs(KC, 1), regs(KC)
                    tr.dma(SP, s_sp2, memT, memT_d[:, :, :], writes=r_mem)
                    rmsnorm(memT, r_mem, 3, lambda kc, tb: (mnT[:, kc, :], r_mn[kc]), nt=256)
                    for t0 in range(0, D, 512):
                        Wv, Wr = load_w(wkv, 0, KC, t0, 512)
                        for j in range(4):
                            m = t0 // 128 + j
                            ps, psr = bank()
                            for kc in range(KC):
                                tr.op(PE, lambda h: h.matmul(ps[:, 0:256], Wv[:, kc, CS(j)], mnT[:, kc, :], start=(kc == 0), stop=(kc == KC - 1)),
                                      reads=[Wr, r_mn[kc]], writes=[psr], inc=(kc == KC - 1))
                            tr.op(ACT, lambda h: h.activation(out=KT[:, m, :], in_=ps[:, 0:256], func=AF.Copy), reads=[psr], writes=[r_KT[m]])

                    def ev_V(c, blk, ps, psr):
                        tr.op(ACT, lambda h: h.activation(out=Vx[:, c, blk * 512:(blk + 1) * 512], in_=ps[:], func=AF.Copy), reads=[psr], writes=[r_Vx[c][blk]])
                    proj_tm(wkv, KC, D, D, lambda kc, c: (mnT[:, kc, CS(c)], r_mn[kc]), ev_V, ntile=2)
                    rmsnorm(xT, xr, 2, h_out)
                    if stop == "x1":
                        dump_exit([(KT.rearrange("p a b -> p (a b)"), 4096), (Vx.rearrange("p a b -> p (a b)"), 4096)])
                    barrier()
                    aoff[0] = xA
                    qTh = alloc([4, T], BF16)
                    aTh = alloc([4, T], BF16)
                    r_qT, r_aT = regs(4, 2), regs(4, 2)
                    PTx = [alloc([512], BF16) for _ in range(4)]
                    r_ptx = regs(4)
                    rden = [alloc([512]) for _ in range(2)]
                    r_rden = regs(2)
                    scale = 512 ** -0.5
                    for hd in range(4):
                        def ev_qx(m, tb, ps, psr):
                            tr.op(ACT, lambda h: h.activation(out=qTh[:, m, TS(tb)], in_=ps[:], func=AF.Copy), reads=[psr], writes=[r_qT[m][tb]])
                        proj_fm(wq, 0, KC, hd * 512, 512, h_rhs, ev_qx)
                        if stop == "x2":
                            dump_exit([(qTh.rearrange("p a b -> p (a b)"), 4096)])
                        for tb in range(2):
                            pts = []
                            for mt in range(2):
                                ps, psr = bank()
                                for j in range(4):
                                    tr.op(PE, lambda h: h.matmul(ps[:], KT[:, hd * 4 + j, CS(mt)], qTh[:, j, TS(tb)], start=(j == 0), stop=(j == 3)),
                                          reads=[r_KT[hd * 4 + j], r_qT[j][tb]], writes=[psr], inc=(j == 3))
                                k = tb * 2 + mt
                                tr.op(ACT, lambda h: h.activation(out=PTx[k], in_=ps[:], func=AF.Exp, scale=scale), reads=[psr], writes=[r_ptx[k]])
                                pts.append(k)
                            pd, pdr = bank()
                            for mt in range(2):
                                tr.op(PE, lambda h: h.matmul(pd[:], ones_bf, PTx[pts[mt]], start=(mt == 0), stop=(mt == 1)),
                                      reads=[r_ptx[pts[mt]], r_c], writes=[pdr], inc=(mt == 1))
                            tr.op(DVE, lambda h: h.reciprocal(rden[tb], pd[:]), reads=[pdr], writes=[r_rden[tb]])
                            for j in range(4):
                                po, por = bank()
                                for mt in range(2):
                                    tr.op(PE, lambda h: h.matmul(po[:], Vx[:, mt, (hd * 4 + j) * 128:(hd * 4 + j + 1) * 128], PTx[pts[mt]], start=(mt == 0), stop=(mt == 1)),
                                          reads=[r_Vx[mt][hd], r_ptx[pts[mt]]], writes=[por], inc=(mt == 1))
                                tr.op(DVE, lambda h: h.tensor_tensor(aTh[:, j, TS(tb)], po[:], rden[tb], ALU.mult), reads=[por, r_rden[tb]], writes=[r_aT[j][tb]])
                        if stop == "x3":
                            dump_exit([(aTh.rearrange("p a b -> p (a b)"), 4096)])
                        Wv, Wr = load_w(wo, hd * 512, 4, 0, D)
                        for m in range(KC):
                            for tb in range(2):
                                po, por = bank()
                                for j in range(4):
                                    tr.op(PE, lambda h: h.matmul(po[:], Wv[:, j, CS(m)], aTh[:, j, TS(tb)], start=(j == 0), stop=(j == 3)),
                                          reads=[Wr, r_aT[j][tb]], writes=[por], inc=(j == 3))
                                tr.op(DVE, lambda h: h.tensor_tensor(xT[:, m, TS(tb)], po[:], xT[:, m, TS(tb)], ALU.add), reads=[por, xr[m][tb]], writes=[xr[m][tb]])
                        if stop == "x4" and hd == 0:
                            dump_exit([(aTh.rearrange("p a b -> p (a b)"), 4096)])
                        if stop == "x5" and hd == 1:
                            dump_exit([(aTh.rearrange("p a b -> p (a b)"), 4096)])
                    barrier()
                    aoff[0] = X_BASE
                    if stop == "xattn":
                        finish(xT, xr)
                    else:
                        rmsnorm(xT, xr, 4, h_out)
                        ffn(xT, xr, hT, hr, f2w1, f2w3, f2w2)
                        rmsnorm(xT, xr, 7, x_out_fn(xT, xr))
                        finish(xT, xr)

        try:
            program()
        except Done:
            pass

        def simulate():
            val = {}
            pc = {k: 0 for k in prog}
            progress = True
            while progress:
                progress = False
                for k, lst in prog.items():
                    while pc[k] < len(lst):
                        rec = lst[pc[k]]
                        if rec["name"] == "wait_ge":
                            sm_, v = rec["a"][0], rec["a"][1]
                            if val.get(id(sm_), 0) < v:
                                break
                        if rec["inc"] is not None:
                            val[id(rec["inc"][0])] = val.get(id(rec["inc"][0]), 0) + rec["inc"][1]
                        pc[k] += 1
                        progress = True
            stuck = {k: (pc[k], len(lst)) for k, lst in prog.items() if pc[k] < len(lst)}
            if stuck:
                msg = []
                for k, (p, n) in stuck.items():
                    rec = prog[k][p]
                    msg.append(f"{k}: pc={p}/{n} waits {rec['a'][0]} >= {rec['a'][1]} (have {val.get(id(rec['a'][0]), 0)})")
                raise RuntimeError("sync plan deadlocks:\n" + "\n".join(msg))
        simulate()

        def replay(key):
            def f(e):
                for rec in prog[key]:
                    ins = getattr(e, rec["name"])(*rec["a"], **rec["kw"])
                    if rec["inc"] is not None:
                        ins.then_inc(*rec["inc"])
            return f
        block.tensor(replay("pe"))
        block.scalar(replay("act"))
        block.vector(replay("dve"))
        block.gpsimd(replay("pool"))
        block.sync(replay("sp"))
    return nc


_NC_CACHE = {}


def _prep_inputs(inputs):
    f = lambda a: np.ascontiguousarray(np.asarray(a, dtype=np.float32))

    def col(v, n):
        return np.asarray(v, np.float32).reshape(n, 128).T
    x = np.asarray(inputs["x"], np.float32)
    mem = np.asarray(inputs["mem"], np.float32)
    gains = np.zeros((128, 8, KC), np.float32)
    for i, k in enumerate(["norm_ffn1", "norm_mix", "norm_xattn", "norm_mem", "norm_ffn2"]):
        gains[:, i, :] = col(inputs[k][0], KC)
    gains[:, 7, :] = col(inputs["norm_final"], KC)
    convp = np.zeros((128, 8, 5), np.float32)
    cw = np.asarray(inputs["mlstm_conv_w"][0], np.float32)
    for j in range(4):
        convp[:, :, j] = col(cw[j], 8)
    convp[:, :, 4] = col(inputs["mlstm_conv_b"][0], 8)
    gb = np.concatenate([np.asarray(inputs["mlstm_ig_bias"][0], np.float32), np.asarray(inputs["mlstm_fg_bias"][0], np.float32)])
    gatebias = np.ascontiguousarray(np.broadcast_to(gb[None, None, :], (128, 8, 8)))
    smallp = np.zeros((128, 4, 8), np.float32)
    smallp[:, 0, :] = col(inputs["mlstm_head_norm"][0], 8)
    smallp[:, 1, :] = col(inputs["hgrn_head_norm"][0], 8)
    smallp[:, 2, :] = col(inputs["hgrn_lb_logits"][0], 8)
    smallp[:, 3, :] = col(inputs["hgrn_lb_logits"][1], 8)
    consts = np.zeros((128, 2, 128), np.float32)
    consts[:, 0, :] = np.triu(np.ones((128, 128), np.float32))
    consts[:, 1, :] = np.eye(128, dtype=np.float32)
    w_in = f(inputs["w_in"][0])
    w_if = np.ascontiguousarray(w_in[:, 3072:3080].reshape(KC, 128, 8).transpose(1, 0, 2))
    shared = {
        "gains": gains, "convp": convp, "gatebias": gatebias, "smallp": smallp, "consts": consts,
        "w_if": w_if, "w_in": w_in,
        "ffn1_w1": f(inputs["ffn1_w1"][0]), "ffn1_w3": f(inputs["ffn1_w3"][0]), "ffn1_w2": f(inputs["ffn1_w2"][0]),
        "ffn2_w1": f(inputs["ffn2_w1"][0]), "ffn2_w3": f(inputs["ffn2_w3"][0]), "ffn2_w2": f(inputs["ffn2_w2"][0]),
        "w_proj_m": f(inputs["w_proj_m"][0]), "w_proj_h": f(inputs["w_proj_h"][0]), "w_out": f(inputs["w_out"][0]),
        "xattn_wq": f(inputs["xattn_wq"][0]), "xattn_wkv": f(inputs["xattn_wkv"][0]), "xattn_wo": f(inputs["xattn_wo"][0]),
    }
    memTs = [np.ascontiguousarray(mem[b].T.reshape(KC, 128, 256).transpose(1, 0, 2)) for b in range(2)]
    in_maps = []
    for c in range(NCORES):
        b, s = c // 4, c % 4
        xs = x[b, s * T:(s + 1) * T, :]
        xT = np.ascontiguousarray(xs.T.reshape(KC, 128, T).transpose(1, 0, 2))
        sel = np.zeros((128, 8), np.float32)
        if s > 0:
            sel[:, s - 1] = 1.0
            sel[:, 4 + s - 1] = 1.0
        m = dict(shared)
        m["xT"] = xT
        m["memT"] = memTs[b]
        m["sel"] = sel
        in_maps.append(m)
    return in_maps


def _assemble(results):
    out = np.empty((2, 4096, D), np.float32)
    for c in range(NCORES):
        b, s = c // 4, c % 4
        oT = np.asarray(results[c]["outT"], np.float32)
        out[b, s * T:(s + 1) * T, :] = oT.transpose(2, 1, 0).reshape(T, D)
    return out


def kernel(**inputs):
    if "nc" not in _NC_CACHE:
        _NC_CACHE["nc"] = build_nc()
    nc = _NC_CACHE["nc"]
    in_maps = _prep_inputs(inputs)
    res = run_bass_kernel_spmd(nc, in_maps, core_ids=list(range(NCORES)))
    return _assemble(res.results)
```

```python
import math
from contextlib import ExitStack
import numpy as np
import concourse.bass as bass
import concourse.mybir as mybir
from concourse.bass_utils import run_bass_kernel_spmd

F32 = mybir.dt.float32
BF16 = mybir.dt.bfloat16
AF = mybir.ActivationFunctionType
ALU = mybir.AluOpType
AX = mybir.AxisListType

D = 2048
T = 1024
KC = 16
DFF = 5632
NSLAB = 11
EPS = 1e-6
NCORES = 8
DIN = 11272
C_MQ, C_MV, C_MO, C_HQ, C_HF, C_HI, C_HG, C_GM, C_GH = 0, 1024, 2048, 3080, 4104, 5128, 6152, 7176, 9224
LNK = -0.5 * math.log(128.0)
GROUPS = [[0, 1, 2, 3], [4, 5, 6, 7]]


class Reg:
    __slots__ = ("lw", "rd", "excl")

    def __init__(self, excl=False):
        self.lw = None
        self.rd = {}
        self.excl = excl


def regs(*shape):
    if len(shape) == 1:
        return [Reg() for _ in range(shape[0])]
    return [regs(*shape[1:]) for _ in range(shape[0])]


def flat(x):
    if isinstance(x, Reg):
        return [x]
    out = []
    for y in x:
        out.extend(flat(y))
    return out


class Eng:
    def __init__(self, name, h, sem, kind):
        self.name, self.h, self.sem, self.kind = name, h, sem, kind
        self.cnt = 0
        self.seen = {}


class Slot:
    def __init__(self, sem):
        self.sem = sem
        self.cnt = 0
        self.kind = "dma"


class Tracker:
    def _waits(self, eng, reads, writes):
        deps = {}

        def add(src, c):
            if deps.get(src, 0) < c:
                deps[src] = c
        for r in reads:
            if r.lw is not None:
                add(*r.lw)
        for w in writes:
            if w.lw is not None:
                add(*w.lw)
            for s, c in w.rd.items():
                add(s, c)
        for src, c in deps.items():
            if src is eng and eng.kind == "pe":
                continue
            if src is eng and c > eng.cnt:
                raise RuntimeError("self wait on pending")
            if eng.seen.get(src, 0) < c:
                eng.h.wait_ge(src.sem, c)
                eng.seen[src] = c

    def op(self, eng, fn, reads=(), writes=(), inc=True):
        reads, writes = flat(reads), flat(writes)
        ex = [r for r in reads if r.excl and r not in writes]
        if ex:
            writes = writes + ex
            reads = [r for r in reads if not r.excl]
        self._waits(eng, reads, writes)
        ins = fn(eng.h)
        if inc:
            ins.then_inc(eng.sem, 1)
            eng.cnt += 1
            tag = eng.cnt
        else:
            tag = eng.cnt + 1
        for r in reads:
            if r.rd.get(eng, 0) < tag:
                r.rd[eng] = tag
        for w in writes:
            w.lw = (eng, tag)
            w.rd = {}
        return ins

    def dma(self, q, slot, out, in_, reads=(), writes=()):
        reads, writes = flat(reads), flat(writes)
        self._waits(q, reads, writes)
        if slot.cnt and q.seen.get(slot, 0) < slot.cnt:
            q.h.wait_ge(slot.sem, slot.cnt)
            q.seen[slot] = slot.cnt
        q.h.dma_start(out=out, in_=in_).then_inc(slot.sem, 16)
        slot.cnt += 16
        for r in reads:
            r.rd[slot] = slot.cnt
        for w in writes:
            w.lw = (slot, slot.cnt)
            w.rd = {}

    def coll(self, q, slot, fn, reads=(), writes=()):
        reads, writes = flat(reads), flat(writes)
        assert slot.cnt == 0
        self._waits(q, reads, writes)
        fn(q.h).then_inc(slot.sem, 1)
        slot.cnt = 1
        for r in reads:
            r.rd[slot] = 1
        for w in writes:
            w.lw = (slot, 1)
            w.rd = {}


def build_nc(stop="full"):
    nc = bass.Bass("TRN2", target_bir_lowering=False)

    def din(name, shape):
        return nc.dram_tensor(name, list(shape), F32, kind="ExternalInput").ap()

    xT_d = din("xT", [128, KC, T])
    memT_d = din("memT", [128, KC, 256])
    sel_d = din("sel", [128, 8])
    gains_d = din("gains", [128, 8, KC])
    convp_d = din("convp", [128, 8, 5])
    gbias_d = din("gatebias", [128, 8, 8])
    smallp_d = din("smallp", [128, 4, 8])
    consts_d = din("consts", [128, 2, 128])
    wif_d = din("w_if", [128, KC, 8])
    w_in = din("w_in", [D, DIN])
    f1w1, f1w3, f1w2 = din("ffn1_w1", [D, DFF]), din("ffn1_w3", [D, DFF]), din("ffn1_w2", [DFF, D])
    f2w1, f2w3, f2w2 = din("ffn2_w1", [D, DFF]), din("ffn2_w3", [D, DFF]), din("ffn2_w2", [DFF, D])
    wpm, wph = din("w_proj_m", [1024, D]), din("w_proj_h", [1024, D])
    wout, wq, wkv, wo = din("w_out", [D, D]), din("xattn_wq", [D, D]), din("xattn_wkv", [D, 2 * D]), din("xattn_wo", [D, D])
    out_d = nc.dram_tensor("outT", [128, KC, T], F32, kind="ExternalOutput").ap()
    xsp_d = nc.dram_tensor("xspill", [128, KC, T], F32).ap()
    halo_in = nc.dram_tensor("halo_in", [128, 24], F32)
    halo_out = nc.dram_tensor("halo_out", [512, 24], F32)
    mst_in = nc.dram_tensor("mst_in", [128, 1032], F32)
    mst_out = nc.dram_tensor("mst_out", [512, 1032], F32)
    hst_in = [nc.dram_tensor(f"hst_in{i}", [128, 516], F32) for i in range(2)]
    hst_out = [nc.dram_tensor(f"hst_out{i}", [512, 516], F32) for i in range(2)]

    es = ExitStack()
    with es:
        NA = 52992
        arena = es.enter_context(nc.sbuf_tensor("arena", [128, NA], F32))
        aoff = [0]

        def alloc(shape, dt=F32):
            n = 1
            for s in shape:
                n *= s
            words = n if dt == F32 else (n + 1) // 2
            words = (words + 15) // 16 * 16
            o = aoff[0]
            assert o + words <= NA, f"arena overflow {o + words}"
            aoff[0] = o + words
            ap = arena[:, o:o + words]
            if dt != F32:
                ap = ap.bitcast(dt)
            ap = ap[:, 0:n]
            if len(shape) == 2:
                ap = ap.rearrange("p (a b) -> p a b", a=shape[0])
            elif len(shape) == 3:
                ap = ap.rearrange("p (a b c) -> p a b c", a=shape[0], b=shape[1])
            return ap

        def sem(name):
            return es.enter_context(nc.semaphore(name))

        tr = Tracker()
        block = es.enter_context(nc.Block())
        prog = {k: [] for k in ("pe", "act", "dve", "pool", "sp")}

        class H:
            def __init__(self, key):
                self.key = key

            def __getattr__(self, name):
                key = self.key

                def call(*a, **kw):
                    rec = {"name": name, "a": a, "kw": kw, "inc": None}
                    prog[key].append(rec)

                    class R:
                        def then_inc(self_, s, v=1):
                            rec["inc"] = (s, v)
                            return self_
                    return R()
                return call

        PE = Eng("pe", H("pe"), sem("s_pe"), "pe")
        ACT = Eng("act", H("act"), sem("s_act"), "act")
        DVE = Eng("dve", H("dve"), sem("s_dve"), "dve")
        POOL = Eng("pool", H("pool"), sem("s_pool"), "pool")
        SP = Eng("sp", H("sp"), sem("s_sp"), "sp")
        engs = [PE, ACT, DVE, POOL, SP]
        slots = []

        def slot(name):
            s = Slot(sem(name))
            slots.append(s)
            return s

        def barrier(with_pool=False):
            for e in engs:
                if e is POOL and not with_pool:
                    continue
                for s in engs + slots:
                    if s is e or not s.cnt:
                        continue
                    if e.seen.get(s, 0) < s.cnt:
                        e.h.wait_ge(s.sem, s.cnt)
                        e.seen[s] = s.cnt

        def TS(tb):
            return slice(tb * 512, (tb + 1) * 512)

        def CS(j):
            return slice(j * 128, (j + 1) * 128)

        NW = 3
        wb = [alloc([8192], BF16) for _ in range(NW)]
        wr = regs(NW)
        ws = [slot(f"s_w{i}") for i in range(NW)]
        wi = [0]
        psb = [es.enter_context(nc.psum_tensor(f"ps{i}", [128, 512], F32)) for i in range(7)]
        pr = [Reg(excl=True) for _ in range(7)]
        pi = [0]
        pbf = es.enter_context(nc.psum_tensor("pbf", [128, 1024], BF16))
        _pb = Reg(excl=True)
        pbfr = [_pb, _pb]
        tmpf = [alloc([512]) for _ in range(3)]
        tmpr = regs(3)
        ti = [0]
        tmpb = [alloc([512], BF16) for _ in range(3)]
        tmpbr = regs(3)
        tbi = [0]
        rstd = alloc([T])
        r_rstd = regs(2)
        cst = alloc([2, 128])
        tri = cst[:, 0, :]
        ident = cst[:, 1, :]
        identb = alloc([128], BF16)
        ones_bf = alloc([128], BF16)
        ones_f = alloc([128])
        gcols = alloc([8, KC])
        cols = alloc([8])
        selt = alloc([8])
        convp = alloc([8, 5])
        gbias = alloc([8, 8])
        smallp = alloc([4, 8])
        lbv = alloc([2, 8])
        wif = alloc([KC, 8], BF16)
        r_c = Reg()
        s_misc = slot("s_misc")
        s_out = slot("s_out")
        s_outs = [slot(f"s_out{i}") for i in range(8)]
        s_sp2 = slot("s_sp2")
        P0_END = aoff[0]

        def bank():
            i = pi[0]
            pi[0] = (i + 1) % 7
            return psb[i], pr[i]

        def tf():
            i = ti[0]
            ti[0] = (i + 1) % 3
            return tmpf[i], tmpr[i]

        def tbf():
            i = tbi[0]
            tbi[0] = (i + 1) % 3
            return tmpb[i], tmpbr[i]

        def wtile():
            i = wi[0]
            wi[0] = (i + 1) % NW
            return wb[i], wr[i], ws[i]

        tr.dma(SP, s_misc, cst, consts_d[:, :, :], writes=[r_c])
        tr.dma(SP, s_misc, gcols, gains_d[:, :, :], writes=[r_c])
        tr.dma(SP, s_misc, selt, sel_d[:, :], writes=[r_c])
        tr.dma(SP, s_misc, convp, convp_d[:, :, :], writes=[r_c])
        tr.dma(SP, s_misc, gbias, gbias_d[:, :, :], writes=[r_c])
        tr.dma(SP, s_misc, smallp, smallp_d[:, :, :], writes=[r_c])
        s_wif = slot("s_wif")
        tr.dma(POOL, s_wif, wif, wif_d[:, :, :], writes=[r_c])
        tr.op(DVE, lambda h: h.memset(ones_bf, 1.0), reads=[r_c], writes=[r_c])
        tr.op(DVE, lambda h: h.memset(ones_f, 1.0), writes=[r_c])
        tr.op(DVE, lambda h: h.memset(cols[:, 0:1], EPS), writes=[r_c])
        tr.op(DVE, lambda h: h.memset(cols[:, 1:2], 1.0), writes=[r_c])
        tr.op(DVE, lambda h: h.memset(cols[:, 2:3], LNK), writes=[r_c])
        tr.op(DVE, lambda h: h.tensor_copy(identb, ident), reads=[r_c], writes=[r_c])
        tr.op(DVE, lambda h: h.tensor_tensor(lbv[:, 0, :], smallp[:, 3, :], smallp[:, 2, :], ALU.subtract), reads=[r_c], writes=[r_c])
        tr.op(ACT, lambda h: h.activation(out=lbv[:, 1, :], in_=lbv[:, 0, :], func=AF.Sigmoid, scale=-1.0), reads=[r_c], writes=[r_c])
        tr.op(ACT, lambda h: h.activation(out=lbv[:, 0, :], in_=lbv[:, 0, :], func=AF.Sigmoid), reads=[r_c], writes=[r_c])
        eps_c, one_c, lnk_c = cols[:, 0:1], cols[:, 1:2], cols[:, 2:3]

        def rmsnorm(xT, xr, gi, out_fn, nt=T, dim=D):
            ntb = max(1, nt // 512)
            w = min(nt, 512)
            for tb in range(ntb):
                sl = slice(tb * w, (tb + 1) * w)
                ps, psr = bank()
                for kc in range(KC):
                    sq, sqr = tbf()
                    tr.op(ACT, lambda h: h.activation(out=sq[:, 0:w], in_=xT[:, kc, sl], func=AF.Square),
                          reads=[xr[kc][tb]], writes=[sqr])
                    tr.op(PE, lambda h: h.matmul(ps[:, 0:w], ones_bf, sq[:, 0:w], start=(kc == 0), stop=(kc == KC - 1)),
                          reads=[sqr, r_c], writes=[psr], inc=True)
                rt, rtr = tf()
                tr.op(ACT, lambda h: h.activation(out=rt[:, 0:w], in_=ps[:, 0:w], func=AF.Sqrt, bias=eps_c, scale=1.0 / dim),
                      reads=[psr, r_c], writes=[rtr])
                tr.op(DVE, lambda h: h.reciprocal(rstd[:, sl], rt[:, 0:w]), reads=[rtr], writes=[r_rstd[tb]])
                for kc in range(KC):
                    o, oreg = out_fn(kc, tb)
                    tr.op(DVE, lambda h: h.scalar_tensor_tensor(o, xT[:, kc, sl], gcols[:, gi, kc:kc + 1], rstd[:, sl], ALU.mult, ALU.mult),
                          reads=[xr[kc][tb], r_c, r_rstd[tb]], writes=[oreg])

        def load_w(Wd, row0, nk, col0, ncols):
            W, Wr, Ws = wtile()
            Wv = W[:, 0:nk * ncols].rearrange("p (k m) -> p k m", k=nk)
            tr.dma(POOL, Ws, Wv, Wd[row0:row0 + nk * 128, col0:col0 + ncols].rearrange("(k p) m -> p k m", p=128), writes=[Wr])
            return Wv, Wr

        def proj_fm(Wd, row0, nk, col0, ncols, rhs_fn, evac):
            for t0 in range(0, ncols, 512):
                Wv, Wr = load_w(Wd, row0, nk, col0 + t0, 512)
                for j in range(4):
                    for tb in range(2):
                        ps, psr = bank()
                        for kc in range(nk):
                            rhs, rreg = rhs_fn(kc, tb)
                            tr.op(PE, lambda h: h.matmul(ps[:], Wv[:, kc, CS(j)], rhs, start=(kc == 0), stop=(kc == nk - 1)),
                                  reads=[Wr, rreg], writes=[psr], inc=(kc == nk - 1))
                        evac(t0 // 128 + j, tb, ps, psr)

        def proj_tm(Wd, nk, col0, ncols, lhs_fn, evac, ntile=8):
            for t0 in range(0, ncols, 512):
                Wv, Wr = load_w(Wd, 0, nk, col0 + t0, 512)
                for c in range(ntile):
                    ps, psr = bank()
                    for kc in range(nk):
                        lhs, lreg = lhs_fn(kc, c)
                        tr.op(PE, lambda h: h.matmul(ps[:], lhs, Wv[:, kc, :], start=(kc == 0), stop=(kc == nk - 1)),
                              reads=[Wr, lreg], writes=[psr], inc=(kc == nk - 1))
                    evac(c, t0 // 512, ps, psr)

        def ffn(xT, xr, hT, hr, w1, w3, w2):
            m0 = aoff[0]
            Gb = [alloc([4, T], BF16) for _ in range(2)]
            Gr = regs(2, 4, 2)
            for s in range(NSLAB):
                W1v, W1r = load_w(w1, 0, KC, s * 512, 512)
                W3v, W3r = load_w(w3, 0, KC, s * 512, 512)
                W2v, W2r = load_w(w2, s * 512, 4, 0, D)
                G, Gg = Gb[s % 2], Gr[s % 2]
                for j in range(4):
                    for tb in range(2):
                        p1, p1r = bank()
                        for kc in range(KC):
                            tr.op(PE, lambda h: h.matmul(p1[:], W1v[:, kc, CS(j)], hT[:, kc, TS(tb)], start=(kc == 0), stop=(kc == KC - 1)),
                                  reads=[W1r, hr[kc][tb]], writes=[p1r], inc=(kc == KC - 1))
                        p3, p3r = bank()
                        for kc in range(KC):
                            tr.op(PE, lambda h: h.matmul(p3[:], W3v[:, kc, CS(j)], hT[:, kc, TS(tb)], start=(kc == 0), stop=(kc == KC - 1)),
                                  reads=[W3r, hr[kc][tb]], writes=[p3r], inc=(kc == KC - 1))
                        s1, s1r = tf()
                        tr.op(ACT, lambda h: h.activation(out=s1, in_=p1[:], func=AF.Silu), reads=[p1r], writes=[s1r])
                        tr.op(DVE, lambda h: h.tensor_tensor(G[:, j, TS(tb)], s1, p3[:], ALU.mult),
                              reads=[s1r, p3r], writes=[Gg[j][tb]])
                for m in range(KC):
                    for tb in range(2):
                        po, por = bank()
                        for kc in range(4):
                            tr.op(PE, lambda h: h.matmul(po[:], W2v[:, kc, CS(m)], G[:, kc, TS(tb)], start=(kc == 0), stop=(kc == 3)),
                                  reads=[W2r, Gg[kc][tb]], writes=[por], inc=(kc == 3))
                        tr.op(DVE, lambda h: h.scalar_tensor_tensor(xT[:, m, TS(tb)], po[:], 0.5, xT[:, m, TS(tb)], ALU.mult, ALU.add),
                              reads=[por, xr[m][tb]], writes=[xr[m][tb]])
            barrier()
            aoff[0] = m0

        def allgather(i_dram, o_dram, src_ap, src_regs, dst_ap, dst_regs, name, nrank, i_view=None):
            r_i, r_o = Reg(), Reg()
            s1, s2, s3 = slot("s_ci_" + name), slot("s_cc_" + name), slot("s_co_" + name)
            tr.dma(SP, s1, i_dram.ap() if i_view is None else i_view, src_ap, reads=src_regs, writes=[r_i])
            tr.coll(POOL, s2, lambda h: h.collective_compute("AllGather", ALU.bypass, replica_groups=GROUPS,
                                                             ins=[i_dram.ap().opt()], outs=[o_dram.ap().opt()]),
                    reads=[r_i], writes=[r_o])
            tr.dma(SP, s3, dst_ap, o_dram.ap()[0:nrank * 128, :].rearrange("(r p) f -> p r f", p=128), reads=[r_o], writes=dst_regs)

        def finish(xT, xr):
            for tb in range(2):
                for q in range(4):
                    tr.dma(SP, s_outs[tb * 4 + q], out_d[:, 4 * q:4 * q + 4, TS(tb)], xT[:, 4 * q:4 * q + 4, TS(tb)], reads=[xr[m][tb] for m in range(4 * q, 4 * q + 4)])
            for so in s_outs:
                prog["sp"].append({"name": "wait_ge", "a": (so.sem, so.cnt), "kw": {}, "inc": None})

        class Done(Exception):
            pass

        def dump_exit(items):
            barrier()
            outflat = out_d.rearrange("p a b -> p (a b)")
            col = 0
            for ap, n in items:
                tr.dma(POOL, s_out, outflat[:, col:col + n], ap)
                col += n
            prog["pool"].append({"name": "wait_ge", "a": (s_out.sem, s_out.cnt), "kw": {}, "inc": None})
            raise Done()

        def program():
            hT = alloc([KC, T], BF16)
            hr = regs(KC, 2)
            PH_BASE = aoff[0]
            xT = alloc([KC, T])
            xr = regs(KC, 2)
            xs = [slot(f"s_x{i}") for i in range(4)]
            for q in range(4):
                tr.dma(SP, xs[q], xT[:, 4 * q:4 * q + 4, :], xT_d[:, 4 * q:4 * q + 4, :], writes=[xr[m] for m in range(4 * q, 4 * q + 4)])

            def h_out(kc, tb):
                return hT[:, kc, TS(tb)], hr[kc][tb]

            def x_out_fn(xT, xr):
                return lambda kc, tb: (xT[:, kc, TS(tb)], xr[kc][tb])

            def h_rhs(kc, tb):
                return hT[:, kc, TS(tb)], hr[kc][tb]

            rmsnorm(xT, xr, 0, h_out)
            ffn(xT, xr, hT, hr, f1w1, f1w3, f1w2)
            if stop == "ffn1":
                finish(xT, xr)
            else:
                rmsnorm(xT, xr, 1, h_out)
                r_xsp = Reg()
                s_spill = slot("s_spill")
                for q in range(4):
                    tr.dma(SP, s_spill, xsp_d[:, 4 * q:4 * q + 4, :], xT[:, 4 * q:4 * q + 4, :], reads=[xr[m] for m in range(4 * q, 4 * q + 4)], writes=[r_xsp])
                barrier()
                aoff[0] = PH_BASE

                ymT = alloc([8, T], BF16)
                r_ym = regs(8, 8)
                MIX_M = aoff[0]

                qkT = alloc([8, T], BF16)
                r_qk = regs(8)
                ktm = alloc([8, 4, 128], BF16)
                r_ktm = regs(4, 2)
                av = alloc([8, 4, 258], BF16)
                r_av = regs(8, 4)
                r_avd = Reg()
                ifr = alloc([8, 8])
                zb = alloc([8, 8])
                ex = alloc([8, 4])
                lt = alloc([8, 4])
                t2 = alloc([8, 4])
                alpha = alloc([8, 4])
                beta = alloc([8, 4])
                ebl = alloc([8, 4])
                totl = alloc([8, 4])
                r_g = regs(10)
                pay = alloc([1032])
                r_pay = regs(5)
                mA = aoff[0]
                qkraw = alloc([8, T + 3])
                r_raw = regs(8, 2)
                r_halo = Reg()
                hgat = alloc([4, 24])
                r_hgat = Reg()
                acc = [alloc([T]) for _ in range(2)]
                r_acc = regs(2)

                def ev_raw(m, tb, ps, psr):
                    tr.op(ACT, lambda h: h.activation(out=qkraw[:, m, 3 + tb * 512:3 + (tb + 1) * 512], in_=ps[:], func=AF.Copy),
                          reads=[psr], writes=[r_raw[m][tb]])
                proj_fm(w_in, 0, KC, C_MQ, 1024, h_rhs, ev_raw)
                allgather(halo_in, halo_out, qkraw[:, :, T:T + 3], [r_raw[m][1] for m in range(8)], hgat, [r_hgat], "halo", 4,
                          i_view=halo_in.ap().rearrange("p (a b) -> p a b", a=8))
                psg, psgr = bank()
                for c in range(8):
                    for kc in range(KC):
                        tr.op(PE, lambda h: h.matmul(psg[:, c * 8:(c + 1) * 8], hT[:, kc, CS(c)], wif[:, kc, :], start=(kc == 0), stop=(kc == KC - 1)),
                              reads=[hr[kc][c // 4], r_c], writes=[psgr], inc=(kc == KC - 1 and c == 7))
                tr.op(ACT, lambda h: h.activation(out=ifr, in_=psg[:, 0:64].rearrange("p (a b) -> p a b", a=8), func=AF.Copy), reads=[psgr], writes=[r_g[0]])
                tr.op(DVE, lambda h: h.tensor_tensor(zb, ifr, gbias, ALU.add), reads=[r_g[0], r_c], writes=[r_g[1]])
                tr.op(ACT, lambda h: h.activation(out=ex, in_=zb[:, :, 4:8], func=AF.Exp, scale=-1.0), reads=[r_g[1]], writes=[r_g[2]])
                tr.op(ACT, lambda h: h.activation(out=lt, in_=ex, func=AF.Ln, bias=one_c), reads=[r_g[2], r_c], writes=[r_g[3]])
                psc, pscr = bank()
                pst, pstr = bank()
                for c in range(8):
                    tr.op(PE, lambda h: h.matmul(psc[:, c * 4:(c + 1) * 4], tri, lt[:, c, :], start=True, stop=True), reads=[r_g[3], r_c], writes=[pscr], inc=(c == 7))
                for c in range(8):
                    tr.op(PE, lambda h: h.matmul(pst[:, c * 4:(c + 1) * 4], ones_f, lt[:, c, :], start=True, stop=True), reads=[r_g[3], r_c], writes=[pstr], inc=(c == 7))
                pscv = psc[:, 0:32].rearrange("p (a b) -> p a b", a=8)
                pstv = pst[:, 0:32].rearrange("p (a b) -> p a b", a=8)
                tr.op(DVE, lambda h: h.tensor_tensor(t2, zb[:, :, 0:4], pscv, ALU.add), reads=[r_g[1], pscr], writes=[r_g[4]])
                tr.op(ACT, lambda h: h.activation(out=alpha, in_=t2, func=AF.Exp, bias=lnk_c), reads=[r_g[4], r_c], writes=[r_g[5]])
                tr.op(ACT, lambda h: h.activation(out=beta, in_=pscv, func=AF.Exp, scale=-1.0), reads=[pscr], writes=[r_g[6]])
                tr.op(ACT, lambda h: h.activation(out=ebl, in_=pstv, func=AF.Exp, scale=-1.0), reads=[pstr], writes=[r_g[7]])
                tr.op(ACT, lambda h: h.activation(out=totl, in_=pstv, func=AF.Copy), reads=[pstr], writes=[r_g[8]])
                tr.op(DVE, lambda h: h.tensor_reduce(pay[:, 1028:1032], totl.rearrange("p c h -> p h c"), AX.X, ALU.add), reads=[r_g[8]], writes=[r_pay[4]])
                if stop == "m1c":
                    dump_exit([(alpha.rearrange("p a b -> p (a b)"), 32), (beta.rearrange("p a b -> p (a b)"), 32), (ebl.rearrange("p a b -> p (a b)"), 32), (pay[:, 1028:1032], 4)])
                tr.op(DVE, lambda h: h.tensor_copy(av[:, :, :, 256], alpha), reads=[r_g[5]], writes=[r_avd])

                def ev_v(c, blk, ps, psr):
                    for hh in range(2):
                        hd = 2 * blk + hh
                        tr.op(DVE, lambda h: h.tensor_scalar(av[:, c, hd, 0:256], ps[:, hh * 256:(hh + 1) * 256], alpha[:, c, hd:hd + 1], None, ALU.mult),
                              reads=[psr, r_g[5]], writes=[r_av[c][hd]])
                proj_tm(w_in, KC, C_MV, 1024, lambda kc, c: (hT[:, kc, CS(c)], hr[kc][c // 4]), ev_v)


                if stop == "m1d":
                    dump_exit([(av.rearrange("p a b c -> p (a b c)"), 8256)])

                halo_dst = qkraw[:, :, 0:3]
                for j in range(4):
                    src = hgat[:, j, :].rearrange("p (a b) -> p a b", a=8)
                    if j == 0:
                        tr.op(DVE, lambda h: h.tensor_scalar(halo_dst, src, selt[:, 0:1], None, ALU.mult), reads=[r_hgat, r_c], writes=[r_halo])
                    else:
                        tr.op(DVE, lambda h: h.scalar_tensor_tensor(halo_dst, src, selt[:, j:j + 1], halo_dst, ALU.mult, ALU.add),
                              reads=[r_hgat, r_c, r_halo], writes=[r_halo])
                for m in range(8):
                    a, ar = acc[m % 2], r_acc[m % 2]
                    tr.op(DVE, lambda h: h.tensor_scalar(a, qkraw[:, m, 0:T], convp[:, m, 0:1], convp[:, m, 4:5], ALU.mult, ALU.add),
                          reads=[r_raw[m], r_halo, r_c], writes=[ar])
                    for j in range(1, 4):
                        tr.op(DVE, lambda h: h.scalar_tensor_tensor(a, qkraw[:, m, j:j + T], convp[:, m, j:j + 1], a, ALU.mult, ALU.add),
                              reads=[r_raw[m], r_halo, r_c, ar], writes=[ar])
                    tr.op(ACT, lambda h: h.activation(out=qkT[:, m, :], in_=a, func=AF.Silu), reads=[ar], writes=[r_qk[m]])
                if stop == "m1":
                    dump_exit([(qkT.rearrange("p a b -> p (a b)"), 8192)])
                for hd in range(4):
                    for half in range(2):
                        for cc in range(4):
                            c = half * 4 + cc
                            tr.op(PE, lambda h: h.transpose(pbf[:, half * 512 + cc * 128:half * 512 + (cc + 1) * 128], qkT[:, 4 + hd, CS(c)], identb),
                                  reads=[r_qk[4 + hd], r_c], writes=[pbfr[half]], inc=(cc == 3))
                        tr.op(ACT, lambda h: h.activation(out=ktm[:, half * 4:half * 4 + 4, hd, :],
                                                          in_=pbf[:, half * 512:(half + 1) * 512].rearrange("p (a b) -> p a b", a=4), func=AF.Copy),
                              reads=[pbfr[half]], writes=[r_ktm[hd][half]])
                if stop == "m1b":
                    dump_exit([(ktm.rearrange("p a b c -> p (a b c)"), 4096)])
                barrier()
                aoff[0] = mA
                Dbf = alloc([4, 8, 258], BF16)
                r_dbf = regs(4, 8)
                Dst = alloc([1028])
                mB = aoff[0]
                gat = alloc([3, 1032])
                r_gat = Reg()
                eB = alloc([3, 4])
                Rb = alloc([1028])
                r_R, r_Dst = Reg(), regs(4)
                def m_chain(Dv, Dr, write_bf):
                    for c in range(8):
                        for hd in range(4):
                            dv = Dv[:, hd * 257:(hd + 1) * 257]
                            if write_bf:
                                tr.op(ACT, lambda h: h.activation(out=Dbf[:, hd, c, 0:257], in_=dv, func=AF.Copy), reads=[Dr[hd]], writes=[r_dbf[hd][c]])
                                if c == 7:
                                    continue
                            pu, pur = bank()
                            tr.op(PE, lambda h: h.matmul(pu[:, 0:257], ktm[:, c, hd, :], av[:, c, hd, 0:257], start=True, stop=True),
                                  reads=[r_ktm[hd][c // 4], r_av[c][hd], r_avd], writes=[pur])
                            tr.op(DVE, lambda h: h.tensor_tensor(dv, pu[:, 0:257], dv, ALU.add), reads=[pur, Dr[hd]], writes=[Dr[hd]])
                            tr.op(DVE, lambda h: h.tensor_scalar(dv, dv, ebl[:, c, hd:hd + 1], None, ALU.mult), reads=[Dr[hd], r_g[7]], writes=[Dr[hd]])

                tr.op(DVE, lambda h: h.memset(pay[:, 0:1028], 0.0), writes=r_pay[0:4])
                m_chain(pay, r_pay, False)
                if stop == "m1e":
                    dump_exit([(pay, 1032)])
                allgather(mst_in, mst_out, pay, r_pay, gat, [r_gat], "mst", 3)
                Wog = [load_w(w_in, 0, KC, C_MO + t * 512, 512) for t in range(2)]

                def og_part(tb, groups, fixed=False):
                    for g in groups:
                        t, j = g // 4, g % 4
                        Wv, Wr = Wog[t]
                        ps, psr = (psb[0], pr[0]) if fixed else bank()
                        for kc in range(KC):
                            tr.op(PE, lambda h: h.matmul(ps[:], Wv[:, kc, CS(j)], hT[:, kc, TS(tb)], start=(kc == 0), stop=(kc == KC - 1)),
                                  reads=[Wr, hr[kc][tb]], writes=[psr], inc=(kc == KC - 1))
                        tr.op(ACT, lambda h: h.activation(out=ymT[:, g, TS(tb)], in_=ps[:], func=AF.Sigmoid), reads=[psr], writes=r_ym[g][tb * 4:tb * 4 + 4])
                og_part(0, list(range(8)))
                tr.op(ACT, lambda h: h.activation(out=eB, in_=gat[:, :, 1028:1032], func=AF.Exp, scale=-1.0), reads=[r_gat], writes=[r_R])
                tr.op(DVE, lambda h: h.tensor_scalar(Dst, gat[:, 0, 0:1028], selt[:, 4:5], None, ALU.mult), reads=[r_gat, r_c], writes=r_Dst)
                for j in (1, 2):
                    prev = gat[:, 0, 0:1028] if j == 1 else Rb
                    for hd in range(4):
                        sl = slice(hd * 257, (hd + 1) * 257)
                        tr.op(DVE, lambda h: h.scalar_tensor_tensor(Rb[:, sl], prev[:, sl], eB[:, j, hd:hd + 1], gat[:, j, sl], ALU.mult, ALU.add),
                              reads=[r_gat, r_R], writes=[r_R])
                    tr.op(DVE, lambda h: h.scalar_tensor_tensor(Dst, Rb, selt[:, 4 + j:5 + j], Dst, ALU.mult, ALU.add), reads=[r_R, r_c] + r_Dst, writes=r_Dst)
                if stop == "m1f":
                    dump_exit([(Dst, 1028)])
                barrier()
                aoff[0] = mB
                m_chain(Dst, r_Dst, True)
                if stop == "m1g":
                    dump_exit([(Dbf.rearrange("p a b c -> p (a b c)"), 8256)])
                nb = [alloc([4, 257]) for _ in range(2)]
                hn = [alloc([4, 256]) for _ in range(2)]
                sm = [alloc([8, 4]) for _ in range(2)]
                PTb = [alloc([4, 128], BF16) for _ in range(2)]
                junk = alloc([256])
                r_nb, r_hn, r_sm, r_pt, r_junk = regs(2, 4), regs(2, 4), regs(2), regs(2), Reg()
                pNs = {}

                def sA1(c):
                    b = c % 2
                    pS, pSr = psb[0], pr[0]
                    for hd in range(4):
                        tr.op(PE, lambda h: h.matmul(pS[:, CS(hd)], qkT[:, 4 + hd, CS(c)], qkT[:, hd, CS(c)], start=True, stop=True),
                              reads=[r_qk[4 + hd], r_qk[hd]], writes=[pSr], inc=(hd == 3))
                    tr.op(DVE, lambda h: h.tensor_tensor(PTb[b], pS[:].rearrange("p (a b) -> p a b", a=4), tri.unsqueeze(1).to_broadcast([128, 4, 128]), ALU.mult),
                          reads=[pSr, r_c], writes=[r_pt[b]])

                def sA2(c):
                    b = c % 2
                    pN = []
                    for hd in range(4):
                        p, prr = psb[1 + hd], pr[1 + hd]
                        pN.append((p, prr))
                        tr.op(PE, lambda h: h.matmul(p[:, 0:257], PTb[b][:, hd, :], av[:, c, hd, 0:257], start=True, stop=False),
                              reads=[r_pt[b], r_av[c][hd], r_avd], writes=[prr], inc=False)
                        tr.op(PE, lambda h: h.matmul(p[:, 0:257], qkT[:, hd, CS(c)], Dbf[:, hd, c, 0:257], start=False, stop=True),
                              reads=[r_qk[hd], r_dbf[hd][c]], writes=[prr], inc=True)
                    pNs[c] = pN

                def sB1a(c):
                    b = c % 2
                    pN = pNs[c]
                    smb = sm[b]
                    tr.op(DVE, lambda h: h.memset(smb[:, 0, :], 0.0), writes=[r_sm[b]])
                    for hd in range(4):
                        p, prr = pN[hd]
                        tr.op(DVE, lambda h: h.tensor_scalar(nb[b][:, hd, :], p[:, 0:257], beta[:, c, hd:hd + 1], None, ALU.mult),
                              reads=[prr, r_g[6]], writes=[r_nb[b][hd]])

                def sB1b(c):
                    b = c % 2
                    smb = sm[b]
                    for hd in range(4):
                        tr.op(ACT, lambda h: h.activation(out=junk, in_=nb[b][:, hd, 0:256], func=AF.Square, accum_out=smb[:, 0, hd:hd + 1]),
                              reads=[r_nb[b][hd], r_sm[b]], writes=[r_sm[b], r_junk])
                    den = nb[b][:, :, 256]
                    tr.op(DVE, lambda h: h.tensor_scalar(smb[:, 1, :], den, -1.0, None, ALU.mult), reads=r_nb[b], writes=[r_sm[b]])
                    tr.op(DVE, lambda h: h.scalar_tensor_tensor(smb[:, 2, :], smb[:, 1, :], 1.0, den, ALU.max, ALU.max), reads=r_nb[b] + [r_sm[b]], writes=[r_sm[b]])
                    tr.op(DVE, lambda h: h.reciprocal(smb[:, 3, :], smb[:, 2, :]), reads=[r_sm[b]], writes=[r_sm[b]])
                    tr.op(DVE, lambda h: h.tensor_tensor(smb[:, 4, :], smb[:, 0, :], smb[:, 3, :], ALU.mult), reads=[r_sm[b]], writes=[r_sm[b]])
                    tr.op(DVE, lambda h: h.tensor_tensor(smb[:, 4, :], smb[:, 4, :], smb[:, 3, :], ALU.mult), reads=[r_sm[b]], writes=[r_sm[b]])
                    tr.op(ACT, lambda h: h.activation(out=smb[:, 5, :], in_=smb[:, 4, :], func=AF.Sqrt, bias=eps_c, scale=1.0 / 256), reads=[r_sm[b], r_c], writes=[r_sm[b]])
                    tr.op(DVE, lambda h: h.reciprocal(smb[:, 6, :], smb[:, 5, :]), reads=[r_sm[b]], writes=[r_sm[b]])
                    tr.op(DVE, lambda h: h.tensor_tensor(smb[:, 7, :], smb[:, 6, :], smb[:, 3, :], ALU.mult), reads=[r_sm[b]], writes=[r_sm[b]])
                    for hd in range(4):
                        tr.op(DVE, lambda h: h.tensor_scalar(hn[b][:, hd, :], nb[b][:, hd, 0:256], smb[:, 7, hd:hd + 1], None, ALU.mult),
                              reads=[r_nb[b][hd], r_sm[b]], writes=[r_hn[b][hd]])

                def sB2(c):
                    b = c % 2
                    for hp in range(2):
                        pT, pTr = psb[5 + hp], pr[5 + hp]
                        for i in range(4):
                            hd, eh = 2 * hp + i // 2, i % 2
                            tr.op(PE, lambda h: h.transpose(pT[:, CS(i)], hn[b][:, hd, CS(eh)], ident), reads=[r_hn[b][hd], r_c], writes=[pTr], inc=(i == 3))
                        for i in range(4):
                            hd, eh = 2 * hp + i // 2, i % 2
                            fe = hd * 2 + eh
                            tr.op(DVE, lambda h: h.scalar_tensor_tensor(ymT[:, fe, CS(c)], pT[:, CS(i)], smallp[:, 0, fe:fe + 1], ymT[:, fe, CS(c)], ALU.mult, ALU.mult),
                                  reads=[pTr, r_c, r_ym[fe][c]], writes=[r_ym[fe][c]])
                        if c < 4:
                            og_part(1, [2 * c + hp], fixed=True)


                sA1(0)
                sA2(0)
                for c in range(8):
                    if c < 7:
                        sA1(c + 1)
                    sB1a(c)
                    if c < 7:
                        sA2(c + 1)
                    sB1b(c)
                    sB2(c)
                if stop == "m2":
                    dump_exit([(ymT.rearrange("p a b -> p (a b)"), 8192)])
                barrier()
                aoff[0] = MIX_M
                yhT = alloc([8, T], BF16)
                r_yh = regs(8, 2)
                MIX_BASE = aoff[0]

                for half in range(2):
                    qt = alloc([4, T], BF16)
                    kt = alloc([4, T], BF16)
                    ktm2 = alloc([8, 4, 128], BF16)
                    vtm = alloc([8, 512], BF16)
                    egl = alloc([4, 8])
                    payh = alloc([516])
                    r_qt, r_kt, r_ktm2, r_vtm, r_egl, r_payh = regs(4, 2), regs(4), regs(4, 2), regs(8, 4), regs(4), regs(5)
                    hA = aoff[0]
                    sg = alloc([4, T])
                    gb_ = alloc([T])
                    Gb_ = alloc([T])
                    e1 = alloc([T])
                    r_sg, r_gb, r_Gb, r_e1 = regs(4, 2), Reg(), Reg(), Reg()

                    def ev_q(m, tb, ps, psr):
                        tr.op(ACT, lambda h: h.activation(out=qt[:, m, TS(tb)], in_=ps[:], func=AF.Silu), reads=[psr], writes=[r_qt[m][tb]])
                    proj_fm(w_in, 0, KC, C_HQ + half * 512, 512, h_rhs, ev_q)

                    def ev_sg(m, tb, ps, psr):
                        tr.op(ACT, lambda h: h.activation(out=sg[:, m, TS(tb)], in_=ps[:], func=AF.Sigmoid), reads=[psr], writes=[r_sg[m][tb]])
                    proj_fm(w_in, 0, KC, C_HF + half * 512, 512, h_rhs, ev_sg)
                    Whi, Whir = load_w(w_in, 0, KC, C_HI + half * 512, 512)

                    def hi_tiles(cs):
                        for c in cs:
                            ps, psr = bank()
                            for kc in range(KC):
                                tr.op(PE, lambda h: h.matmul(ps[:], hT[:, kc, CS(c)], Whi[:, kc, :], start=(kc == 0), stop=(kc == KC - 1)),
                                      reads=[Whir, hr[kc][c // 4]], writes=[psr], inc=(kc == KC - 1))
                            tr.op(ACT, lambda h: h.activation(out=vtm[:, c, :], in_=ps[:], func=AF.Copy), reads=[psr], writes=r_vtm[c])
                    for i in range(4):
                        gh = half * 4 + i
                        f_ = sg[:, i, :]
                        tr.op(DVE, lambda h: h.tensor_scalar(f_, f_, lbv[:, 1, gh:gh + 1], lbv[:, 0, gh:gh + 1], ALU.mult, ALU.add), reads=r_sg[i] + [r_c], writes=r_sg[i])
                        tr.op(ACT, lambda h: h.activation(out=gb_, in_=f_, func=AF.Ln), reads=r_sg[i], writes=[r_gb])
                        for c in range(8):
                            tr.op(DVE, lambda h: h.tensor_tensor_scan(Gb_[:, CS(c)], ones_f, gb_[:, CS(c)], 0.0, ALU.mult, ALU.add), reads=[r_gb, r_c], writes=[r_Gb])
                        tr.op(DVE, lambda h: h.tensor_reduce(payh[:, 512 + i:513 + i], gb_, AX.X, ALU.add), reads=[r_gb], writes=[r_payh[4]])
                        tr.op(DVE, lambda h: h.tensor_scalar(f_, f_, -1.0, 1.0, ALU.mult, ALU.add), reads=r_sg[i], writes=r_sg[i])
                        tr.op(ACT, lambda h: h.activation(out=e1, in_=Gb_, func=AF.Exp, scale=-1.0), reads=[r_Gb], writes=[r_e1])
                        tr.op(DVE, lambda h: h.tensor_tensor(kt[:, i, :], f_, e1, ALU.mult), reads=r_sg[i] + [r_e1], writes=[r_kt[i]])
                        tr.op(ACT, lambda h: h.activation(out=egl[:, i, :], in_=Gb_.rearrange("p (c t) -> p c t", t=128)[:, :, 127], func=AF.Exp), reads=[r_Gb], writes=[r_egl[i]])
                        tr.op(ACT, lambda h: h.activation(out=e1, in_=Gb_, func=AF.Exp, bias=lnk_c), reads=[r_Gb, r_c, r_kt[i]], writes=[r_e1])
                        tr.op(DVE, lambda h: h.tensor_tensor(qt[:, i, :], qt[:, i, :], e1, ALU.mult), reads=r_qt[i] + [r_e1], writes=r_qt[i])
                        hi_tiles([2 * i, 2 * i + 1])
                    for i in range(4):
                        for hf in range(2):
                            for cc in range(4):
                                c = hf * 4 + cc
                                tr.op(PE, lambda h: h.transpose(pbf[:, hf * 512 + cc * 128:hf * 512 + (cc + 1) * 128], kt[:, i, CS(c)], identb),
                                      reads=[r_kt[i], r_c], writes=[pbfr[hf]], inc=(cc == 3))
                            tr.op(ACT, lambda h: h.activation(out=ktm2[:, hf * 4:hf * 4 + 4, i, :],
                                                              in_=pbf[:, hf * 512:(hf + 1) * 512].rearrange("p (a b) -> p a b", a=4), func=AF.Copy),
                                  reads=[pbfr[hf]], writes=[r_ktm2[i][hf]])
                    barrier()
                    aoff[0] = hA
                    oT = alloc([4, T])
                    r_oT = regs(4, 8)
                    Sbf = alloc([4, 8, 128], BF16)
                    r_sbf = regs(4, 8)
                    gath = alloc([3, 516])
                    r_gath = Reg()
                    eBh = alloc([3, 4])
                    Rh = alloc([512])
                    Sst = alloc([512])
                    r_Rh, r_Sst = Reg(), regs(4)
                    Pm = [alloc([4, 128], BF16) for _ in range(2)]
                    r_pm = regs(2)

                    def h_chain(Sv, Sr, write_bf):
                        for c in range(8):
                            if write_bf:
                                for i in range(4):
                                    tr.op(ACT, lambda h: h.activation(out=Sbf[:, i, c, :], in_=Sv[:, CS(i)], func=AF.Copy), reads=[Sr[i]], writes=[r_sbf[i][c]])
                                if c == 7:
                                    continue
                            pu, pur = bank()
                            for i in range(4):
                                tr.op(PE, lambda h: h.matmul(pu[:, CS(i)], ktm2[:, c, i, :], vtm[:, c, CS(i)], start=True, stop=True),
                                      reads=[r_ktm2[i][c // 4], r_vtm[c][i]], writes=[pur], inc=(i == 3))
                            for i in range(4):
                                sv = Sv[:, CS(i)]
                                tr.op(DVE, lambda h: h.tensor_tensor(sv, pu[:, CS(i)], sv, ALU.add), reads=[pur, Sr[i]], writes=[Sr[i]])
                                tr.op(DVE, lambda h: h.tensor_scalar(sv, sv, egl[:, i, c:c + 1], None, ALU.mult), reads=[Sr[i], r_egl[i]], writes=[Sr[i]])

                    tr.op(DVE, lambda h: h.memset(payh[:, 0:512], 0.0), writes=r_payh[0:4])
                    h_chain(payh, r_payh, False)
                    allgather(hst_in[half], hst_out[half], payh, r_payh, gath, [r_gath], f"hst{half}", 3)

                    def ev_g(m, tb, ps, psr):
                        tr.op(ACT, lambda h: h.activation(out=yhT[:, half * 4 + m, TS(tb)], in_=ps[:], func=AF.Silu), reads=[psr], writes=[r_yh[half * 4 + m][tb]])
                    proj_fm(w_in, 0, KC, C_HG + half * 512, 512, h_rhs, ev_g)
                    tr.op(ACT, lambda h: h.activation(out=eBh, in_=gath[:, :, 512:516], func=AF.Exp), reads=[r_gath], writes=[r_Rh])
                    tr.op(DVE, lambda h: h.tensor_scalar(Sst, gath[:, 0, 0:512], selt[:, 4:5], None, ALU.mult), reads=[r_gath, r_c], writes=r_Sst)
                    for j in (1, 2):
                        prev = gath[:, 0, 0:512] if j == 1 else Rh
                        for i in range(4):
                            tr.op(DVE, lambda h: h.scalar_tensor_tensor(Rh[:, CS(i)], prev[:, CS(i)], eBh[:, j, i:i + 1], gath[:, j, CS(i)], ALU.mult, ALU.add),
                                  reads=[r_gath, r_Rh], writes=[r_Rh])
                        tr.op(DVE, lambda h: h.scalar_tensor_tensor(Sst, Rh, selt[:, 4 + j:5 + j], Sst, ALU.mult, ALU.add), reads=[r_Rh, r_c] + r_Sst, writes=r_Sst)
                    h_chain(Sst, r_Sst, True)
                    for c in range(8):
                        b = c % 2
                        pA, pAr = bank()
                        for i in range(4):
                            tr.op(PE, lambda h: h.matmul(pA[:, CS(i)], kt[:, i, CS(c)], qt[:, i, CS(c)], start=True, stop=True),
                                  reads=[r_kt[i], r_qt[i][c // 4]], writes=[pAr], inc=(i == 3))
                        tr.op(DVE, lambda h: h.tensor_tensor(Pm[b], pA[:].rearrange("p (a b) -> p a b", a=4), tri.unsqueeze(1).to_broadcast([128, 4, 128]), ALU.mult),
                              reads=[pAr, r_c], writes=[r_pm[b]])
                        pO, pOr = bank()
                        for i in range(4):
                            tr.op(PE, lambda h: h.matmul(pO[:, CS(i)], vtm[:, c, CS(i)], Pm[b][:, i, :], start=True, stop=False),
                                  reads=[r_vtm[c][i], r_pm[b]], writes=[pOr], inc=False)
                            tr.op(PE, lambda h: h.matmul(pO[:, CS(i)], Sbf[:, i, c, :], qt[:, i, CS(c)], start=False, stop=True),
                                  reads=[r_sbf[i][c], r_qt[i][c // 4]], writes=[pOr], inc=(i == 3))
                        tr.op(ACT, lambda h: h.activation(out=oT[:, :, CS(c)], in_=pO[:].rearrange("p (a b) -> p a b", a=4), func=AF.Copy),
                              reads=[pOr], writes=[r_oT[i][c] for i in range(4)])
                    for i in range(4):
                        gh = half * 4 + i
                        for tb in range(2):
                            sq, sqr = tbf()
                            tr.op(ACT, lambda h: h.activation(out=sq, in_=oT[:, i, TS(tb)], func=AF.Square), reads=r_oT[i][tb * 4:tb * 4 + 4], writes=[sqr])
                            ps, psr = bank()
                            tr.op(PE, lambda h: h.matmul(ps[:], ones_bf, sq, start=True, stop=True), reads=[sqr, r_c], writes=[psr])
                            rt, rtr = tf()
                            tr.op(ACT, lambda h: h.activation(out=rt, in_=ps[:], func=AF.Sqrt, bias=eps_c, scale=1.0 / 128), reads=[psr, r_c], writes=[rtr])
                            tr.op(DVE, lambda h: h.reciprocal(rt, rt), reads=[rtr], writes=[rtr])
                            y1, y1r = tf()
                            tr.op(DVE, lambda h: h.scalar_tensor_tensor(y1, oT[:, i, TS(tb)], smallp[:, 1, gh:gh + 1], rt, ALU.mult, ALU.mult),
                                  reads=r_oT[i][tb * 4:tb * 4 + 4] + [r_c, rtr], writes=[y1r])
                            tr.op(DVE, lambda h: h.tensor_tensor(yhT[:, gh, TS(tb)], y1, yhT[:, gh, TS(tb)], ALU.mult), reads=[y1r, r_yh[gh][tb]], writes=[r_yh[gh][tb]])
                    barrier()
                    aoff[0] = MIX_BASE

                if stop == "h":
                    dump_exit([(ymT.rearrange("p a b -> p (a b)"), 8192), (yhT.rearrange("p a b -> p (a b)"), 8192)])
                aoff[0] = NA - 8192 - 16
                mgT = alloc([KC, T], BF16)
                r_mg = regs(KC, 2)
                aoff[0] = MIX_BASE
                sgb = [alloc([4, T], BF16) for _ in range(2)]
                r_sgb = regs(2, 4, 2)
                rA, rB, rC = wr[0], wr[1], [wr[2], wr[2]]
                steps = [(wpm, C_GM, blk) for blk in range(4)] + [(wph, C_GH, blk) for blk in range(4)]
                Wg_v, Wp_v = {}, {}

                def mg_loads(st):
                    srcp, cg, blk = steps[st]
                    gb, gr, gs = (wb[0], rA, ws[0]) if st % 2 == 0 else (wb[1], rB, ws[1])
                    Wg = gb[:, 0:8192].rearrange("p (k m) -> p k m", k=KC)
                    tr.dma(POOL, gs, Wg, w_in[0:D, cg + blk * 512:cg + (blk + 1) * 512].rearrange("(k p) m -> p k m", p=128), writes=[gr])
                    ph = st % 2
                    Wp = wb[2][:, ph * 4096:(ph + 1) * 4096].rearrange("p (k m) -> p k m", k=8)
                    tr.dma(POOL, ws[2], Wp, srcp[0:1024, blk * 512:(blk + 1) * 512].rearrange("(k p) m -> p k m", p=128), writes=[rC[ph]])
                    Wg_v[st], Wp_v[st] = (Wg, gr), (Wp, rC[ph])

                mg_loads(0)
                for st in range(8):
                    if st + 1 < 8:
                        mg_loads(st + 1)
                    (Wg, Wgr), (Wp, Wpr) = Wg_v[st], Wp_v[st]
                    blk = steps[st][2]
                    second = st >= 4
                    sg_, sgr_ = sgb[st % 2], r_sgb[st % 2]
                    for j in range(4):
                        for tb in range(2):
                            p2, p2r = bank()
                            for kc in range(KC):
                                tr.op(PE, lambda h: h.matmul(p2[:], Wg[:, kc, CS(j)], hT[:, kc, TS(tb)], start=(kc == 0), stop=(kc == KC - 1)),
                                      reads=[Wgr, hr[kc][tb]], writes=[p2r], inc=(kc == KC - 1))
                            tr.op(ACT, lambda h: h.activation(out=sg_[:, j, TS(tb)], in_=p2[:], func=AF.Sigmoid), reads=[p2r], writes=[sgr_[j][tb]])
                    for j in range(4):
                        m = blk * 4 + j
                        for tb in range(2):
                            p1, p1r = bank()
                            for kc in range(8):
                                if second:
                                    rhs, rr = yhT[:, kc, TS(tb)], [r_yh[kc][tb]]
                                else:
                                    rhs, rr = ymT[:, kc, TS(tb)], r_ym[kc][tb * 4:tb * 4 + 4]
                                tr.op(PE, lambda h: h.matmul(p1[:], Wp[:, kc, CS(j)], rhs, start=(kc == 0), stop=(kc == 7)),
                                      reads=[Wpr] + rr, writes=[p1r], inc=(kc == 7))
                            if not second:
                                tr.op(DVE, lambda h: h.tensor_tensor(mgT[:, m, TS(tb)], sg_[:, j, TS(tb)], p1[:], ALU.mult), reads=[sgr_[j][tb], p1r], writes=[r_mg[m][tb]])
                            else:
                                s2, s2r = tf()
                                tr.op(DVE, lambda h: h.tensor_tensor(s2, sg_[:, j, TS(tb)], p1[:], ALU.mult), reads=[sgr_[j][tb], p1r], writes=[s2r])
                                tr.op(DVE, lambda h: h.tensor_tensor(mgT[:, m, TS(tb)], s2, mgT[:, m, TS(tb)], ALU.add), reads=[s2r, r_mg[m][tb]], writes=[r_mg[m][tb]])
                barrier()

                aoff[0] = P0_END
                xT = alloc([KC, T])
                xr = regs(KC, 2)
                for q in range(4):
                    tr.dma(SP, xs[q], xT[:, 4 * q:4 * q + 4, :], xsp_d[:, 4 * q:4 * q + 4, :], reads=[r_xsp], writes=[xr[m] for m in range(4 * q, 4 * q + 4)])

                def ev_addx(m, tb, ps, psr):
                    tr.op(DVE, lambda h: h.tensor_tensor(xT[:, m, TS(tb)], ps[:], xT[:, m, TS(tb)], ALU.add), reads=[psr, xr[m][tb]], writes=[xr[m][tb]])
                proj_fm(wout, 0, KC, 0, D, lambda kc, tb: (mgT[:, kc, TS(tb)], r_mg[kc][tb]), ev_addx)
                barrier()
                if stop == "mix":
                    finish(xT, xr)
                else:
                    hT = alloc([KC, T], BF16)
                    hr = regs(KC, 2)
                    X_BASE = aoff[0]
                    KT = alloc([KC, 256], BF16)
                    Vx = alloc([2, D], BF16)
                    r_KT, r_Vx = regs(KC), regs(2, 4)
                    xA = aoff[0]
                    memT = alloc([KC, 256])
                    mnT = alloc([KC, 256], BF16)
                    r_mem, r_mn = regs(KC, 1), regs(KC)
                    tr.dma(SP, s_sp2, memT, memT_d[:, :, :], writes=r_mem)
                    rmsnorm(memT, r_mem, 3, lambda kc, tb: (mnT[:, kc, :], r_mn[kc]), nt=256)
                    for t0 in range(0, D, 512):
                        Wv, Wr = load_w(wkv, 0, KC, t0, 512)
                        for j in range(4):
                            m = t0 // 128 + j
                            ps, psr = bank()
                            for kc in range(KC):
                                tr.op(PE, lambda h: h.matmul(ps[:, 0:256], Wv[:, kc, CS(j)], mnT[:, kc, :], start=(kc == 0), stop=(kc == KC - 1)),
                                      reads=[Wr, r_mn[kc]], writes=[psr], inc=(kc == KC - 1))
                            tr.op(ACT, lambda h: h.activation(out=KT[:, m, :], in_=ps[:, 0:256], func=AF.Copy), reads=[psr], writes=[r_KT[m]])

                    def ev_V(c, blk, ps, psr):
                        tr.op(ACT, lambda h: h.activation(out=Vx[:, c, blk * 512:(blk + 1) * 512], in_=ps[:], func=AF.Copy), reads=[psr], writes=[r_Vx[c][blk]])
                    proj_tm(wkv, KC, D, D, lambda kc, c: (mnT[:, kc, CS(c)], r_mn[kc]), ev_V, ntile=2)
                    rmsnorm(xT, xr, 2, h_out)
                    if stop == "x1":
                        dump_exit([(KT.rearrange("p a b -> p (a b)"), 4096), (Vx.rearrange("p a b -> p (a b)"), 4096)])
                    barrier()
                    aoff[0] = xA
                    qTh = alloc([4, T], BF16)
                    aTh = alloc([4, T], BF16)
                    r_qT, r_aT = regs(4, 2), regs(4, 2)
                    PTx = [alloc([512], BF16) for _ in range(4)]
                    r_ptx = regs(4)
                    rden = [alloc([512]) for _ in range(2)]
                    r_rden = regs(2)
                    scale = 512 ** -0.5
                    for hd in range(4):
                        def ev_qx(m, tb, ps, psr):
                            tr.op(ACT, lambda h: h.activation(out=qTh[:, m, TS(tb)], in_=ps[:], func=AF.Copy), reads=[psr], writes=[r_qT[m][tb]])
                        proj_fm(wq, 0, KC, hd * 512, 512, h_rhs, ev_qx)
                        if stop == "x2":
                            dump_exit([(qTh.rearrange("p a b -> p (a b)"), 4096)])
                        for tb in range(2):
                            pts = []
                            for mt in range(2):
                                ps, psr = bank()
                                for j in range(4):
                                    tr.op(PE, lambda h: h.matmul(ps[:], KT[:, hd * 4 + j, CS(mt)], qTh[:, j, TS(tb)], start=(j == 0), stop=(j == 3)),
                                          reads=[r_KT[hd * 4 + j], r_qT[j][tb]], writes=[psr], inc=(j == 3))
                                k = tb * 2 + mt
                                tr.op(ACT, lambda h: h.activation(out=PTx[k], in_=ps[:], func=AF.Exp, scale=scale), reads=[psr], writes=[r_ptx[k]])
                                pts.append(k)
                            pd, pdr = bank()
                            for mt in range(2):
                                tr.op(PE, lambda h: h.matmul(pd[:], ones_bf, PTx[pts[mt]], start=(mt == 0), stop=(mt == 1)),
                                      reads=[r_ptx[pts[mt]], r_c], writes=[pdr], inc=(mt == 1))
                            tr.op(DVE, lambda h: h.reciprocal(rden[tb], pd[:]), reads=[pdr], writes=[r_rden[tb]])
                            for j in range(4):
                                po, por = bank()
                                for mt in range(2):
                                    tr.op(PE, lambda h: h.matmul(po[:], Vx[:, mt, (hd * 4 + j) * 128:(hd * 4 + j + 1) * 128], PTx[pts[mt]], start=(mt == 0), stop=(mt == 1)),
                                          reads=[r_Vx[mt][hd], r_ptx[pts[mt]]], writes=[por], inc=(mt == 1))
                                tr.op(DVE, lambda h: h.tensor_tensor(aTh[:, j, TS(tb)], po[:], rden[tb], ALU.mult), reads=[por, r_rden[tb]], writes=[r_aT[j][tb]])
                        if stop == "x3":
                            dump_exit([(aTh.rearrange("p a b -> p (a b)"), 4096)])
                        Wv, Wr = load_w(wo, hd * 512, 4, 0, D)
                        for m in range(KC):
                            for tb in range(2):
                                po, por = bank()
                                for j in range(4):
                                    tr.op(PE, lambda h: h.matmul(po[:], Wv[:, j, CS(m)], aTh[:, j, TS(tb)], start=(j == 0), stop=(j == 3)),
                                          reads=[Wr, r_aT[j][tb]], writes=[por], inc=(j == 3))
                                tr.op(DVE, lambda h: h.tensor_tensor(xT[:, m, TS(tb)], po[:], xT[:, m, TS(tb)], ALU.add), reads=[por, xr[m][tb]], writes=[xr[m][tb]])
                        if stop == "x4" and hd == 0:
                            dump_exit([(aTh.rearrange("p a b -> p (a b)"), 4096)])
                        if stop == "x5" and hd == 1:
                            dump_exit([(aTh.rearrange("p a b -> p (a b)"), 4096)])
                    barrier()
                    aoff[0] = X_BASE
                    if stop == "xattn":
                        finish(xT, xr)
                    else:
                        rmsnorm(xT, xr, 4, h_out)
                        ffn(xT, xr, hT, hr, f2w1, f2w3, f2w2)
                        rmsnorm(xT, xr, 7, x_out_fn(xT, xr))
                        finish(xT, xr)

        try:
            program()
        except Done:
            pass

        def simulate():
            val = {}
            pc = {k: 0 for k in prog}
            progress = True
            while progress:
                progress = False
                for k, lst in prog.items():
                    while pc[k] < len(lst):
                        rec = lst[pc[k]]
                        if rec["name"] == "wait_ge":
                            sm_, v = rec["a"][0], rec["a"][1]
                            if val.get(id(sm_), 0) < v:
                                break
                        if rec["inc"] is not None:
                            val[id(rec["inc"][0])] = val.get(id(rec["inc"][0]), 0) + rec["inc"][1]
                        pc[k] += 1
                        progress = True
            stuck = {k: (pc[k], len(lst)) for k, lst in prog.items() if pc[k] < len(lst)}
            if stuck:
                msg = []
                for k, (p, n) in stuck.items():
                    rec = prog[k][p]
                    msg.append(f"{k}: pc={p}/{n} waits {rec['a'][0]} >= {rec['a'][1]} (have {val.get(id(rec['a'][0]), 0)})")
                raise RuntimeError("sync plan deadlocks:\n" + "\n".join(msg))
        simulate()

        def replay(key):
            def f(e):
                for rec in prog[key]:
                    ins = getattr(e, rec["name"])(*rec["a"], **rec["kw"])
                    if rec["inc"] is not None:
                        ins.then_inc(*rec["inc"])
            return f
        block.tensor(replay("pe"))
        block.scalar(replay("act"))
        block.vector(replay("dve"))
        block.gpsimd(replay("pool"))
        block.sync(replay("sp"))
    return nc


_NC_CACHE = {}


def _prep_inputs(inputs):
    f = lambda a: np.ascontiguousarray(np.asarray(a, dtype=np.float32))

    def col(v, n):
        return np.asarray(v, np.float32).reshape(n, 128).T
    x = np.asarray(inputs["x"], np.float32)
    mem = np.asarray(inputs["mem"], np.float32)
    gains = np.zeros((128, 8, KC), np.float32)
    for i, k in enumerate(["norm_ffn1", "norm_mix", "norm_xattn", "norm_mem", "norm_ffn2"]):
        gains[:, i, :] = col(inputs[k][0], KC)
    gains[:, 7, :] = col(inputs["norm_final"], KC)
    convp = np.zeros((128, 8, 5), np.float32)
    cw = np.asarray(inputs["mlstm_conv_w"][0], np.float32)
    for j in range(4):
        convp[:, :, j] = col(cw[j], 8)
    convp[:, :, 4] = col(inputs["mlstm_conv_b"][0], 8)
    gb = np.concatenate([np.asarray(inputs["mlstm_ig_bias"][0], np.float32), np.asarray(inputs["mlstm_fg_bias"][0], np.float32)])
    gatebias = np.ascontiguousarray(np.broadcast_to(gb[None, None, :], (128, 8, 8)))
    smallp = np.zeros((128, 4, 8), np.float32)
    smallp[:, 0, :] = col(inputs["mlstm_head_norm"][0], 8)
    smallp[:, 1, :] = col(inputs["hgrn_head_norm"][0], 8)
    smallp[:, 2, :] = col(inputs["hgrn_lb_logits"][0], 8)
    smallp[:, 3, :] = col(inputs["hgrn_lb_logits"][1], 8)
    consts = np.zeros((128, 2, 128), np.float32)
    consts[:, 0, :] = np.triu(np.ones((128, 128), np.float32))
    consts[:, 1, :] = np.eye(128, dtype=np.float32)
    w_in = f(inputs["w_in"][0])
    w_if = np.ascontiguousarray(w_in[:, 3072:3080].reshape(KC, 128, 8).transpose(1, 0, 2))
    shared = {
        "gains": gains, "convp": convp, "gatebias": gatebias, "smallp": smallp, "consts": consts,
        "w_if": w_if, "w_in": w_in,
        "ffn1_w1": f(inputs["ffn1_w1"][0]), "ffn1_w3": f(inputs["ffn1_w3"][0]), "ffn1_w2": f(inputs["ffn1_w2"][0]),
        "ffn2_w1": f(inputs["ffn2_w1"][0]), "ffn2_w3": f(inputs["ffn2_w3"][0]), "ffn2_w2": f(inputs["ffn2_w2"][0]),
        "w_proj_m": f(inputs["w_proj_m"][0]), "w_proj_h": f(inputs["w_proj_h"][0]), "w_out": f(inputs["w_out"][0]),
        "xattn_wq": f(inputs["xattn_wq"][0]), "xattn_wkv": f(inputs["xattn_wkv"][0]), "xattn_wo": f(inputs["xattn_wo"][0]),
    }
    memTs = [np.ascontiguousarray(mem[b].T.reshape(KC, 128, 256).transpose(1, 0, 2)) for b in range(2)]
    in_maps = []
    for c in range(NCORES):
        b, s = c // 4, c % 4
        xs = x[b, s * T:(s + 1) * T, :]
        xT = np.ascontiguousarray(xs.T.reshape(KC, 128, T).transpose(1, 0, 2))
        sel = np.zeros((128, 8), np.float32)
        if s > 0:
            sel[:, s - 1] = 1.0
            sel[:, 4 + s - 1] = 1.0
        m = dict(shared)
        m["xT"] = xT
        m["memT"] = memTs[b]
        m["sel"] = sel
        in_maps.append(m)
    return in_maps


def _assemble(results):
    out = np.empty((2, 4096, D), np.float32)
    for c in range(NCORES):
        b, s = c // 4, c % 4
        oT = np.asarray(results[c]["outT"], np.float32)
        out[b, s * T:(s + 1) * T, :] = oT.transpose(2, 1, 0).reshape(T, D)
    return out


def kernel(**inputs):
    if "nc" not in _NC_CACHE:
        _NC_CACHE["nc"] = build_nc()
    nc = _NC_CACHE["nc"]
    in_maps = _prep_inputs(inputs)
    res = run_bass_kernel_spmd(nc, in_maps, core_ids=list(range(NCORES)))
    return _assemble(res.results)
```
